# Optimizing a Trainium2 kernel written in Bass

```python
import jax
import jax.numpy as jnp
from jax import lax
import numpy as np

D_MODEL = 1024
BATCH = 4
SEQ = 4096
DEPTH = 4

GRID_W = 64
CTX_LEN = 256
N_MOD = 9
ATTN_HEAD_DIM = 64
ATTN_WIDTH = D_MODEL // 2
ATTN_HEADS = ATTN_WIDTH // ATTN_HEAD_DIM
RWKV_HEAD_DIM = 64
RWKV_WIDTH = D_MODEL - ATTN_WIDTH
RWKV_HEADS = RWKV_WIDTH // RWKV_HEAD_DIM
NA_ROWS = 8
NA_COLS = 16
DECAY_LORA = 64
ICLR_LORA = 64
GATE_LORA = 128
D_FF = ((8 * D_MODEL // 3 + 127) // 128) * 128
RWKV_IN = 3 * RWKV_WIDTH + DECAY_LORA + ICLR_LORA + GATE_LORA
D_IN = 3 * ATTN_WIDTH + RWKV_IN
RWKV_SPLITS = (RWKV_WIDTH, 2 * RWKV_WIDTH, 3 * RWKV_WIDTH,
               3 * RWKV_WIDTH + DECAY_LORA, 3 * RWKV_WIDTH + DECAY_LORA + ICLR_LORA)
ATTN_SCALE = ATTN_HEAD_DIM ** -0.5
RMS_EPS = 1e-6
LNX_EPS = 64e-5
L2_EPS = 1e-24
NEG_INF = -1e30

kernel_name = 'hybrid_na_rwkv7_macaron_dit'


def rms_norm(x, gain):
    xf = x.astype(jnp.float32)
    y = xf * lax.rsqrt(jnp.mean(xf * xf, axis=-1, keepdims=True) + RMS_EPS)
    return (y * gain.astype(jnp.float32)).astype(x.dtype)


def modulate(x, shift, scale):
    return x * (1 + scale) + shift


def swiglu(x, w_up, w_down):
    gate, up = jnp.split(x @ w_up, 2, axis=-1)
    return (jax.nn.silu(gate) * up) @ w_down


def attn_heads(p, q_gain, k_gain):
    b, n, _ = p.shape
    q, k, v = jnp.split(p, 3, axis=-1)
    shape = (b, n, ATTN_HEADS, ATTN_HEAD_DIM)
    q = rms_norm(q.reshape(shape), q_gain) * ATTN_SCALE
    k = rms_norm(k.reshape(shape), k_gain)
    return q, k, v.reshape(shape)


def neighbourhood_attention(q, k, v, k_ctx, v_ctx, rel_bias):
    b, t, h, d = q.shape
    rows = t // GRID_W
    kr = min(NA_ROWS, rows)
    qg = q.reshape(b, rows, GRID_W, h, d)
    kg = k.reshape(b, rows, GRID_W, h, d)
    vg = v.reshape(b, rows, GRID_W, h, d)
    cols = jnp.arange(GRID_W)
    col_start = jnp.clip(cols - NA_COLS // 2, 0, GRID_W - NA_COLS)
    in_window = (cols[None, :] >= col_start[:, None]) & (cols[None, :] < col_start[:, None] + NA_COLS)
    dc = jnp.clip(cols[None, :] - cols[:, None], -(NA_COLS - 1), NA_COLS - 1) + (NA_COLS - 1)

    def row_block(i):
        start = jnp.clip(i - kr // 2, 0, rows - kr)
        q_i = lax.dynamic_index_in_dim(qg, i, axis=1, keepdims=False)
        k_i = lax.dynamic_slice_in_dim(kg, start, kr, axis=1)
        v_i = lax.dynamic_slice_in_dim(vg, start, kr, axis=1)
        dr = start + jnp.arange(kr) - i + (NA_ROWS - 1)
        bias = rel_bias[:, dr[None, :, None], dc[:, None, :]].astype(jnp.float32)
        bias = jnp.where(in_window[None, :, None, :], bias, NEG_INF)
        s_loc = jnp.einsum('bqhd,brwhd->bhqrw', q_i, k_i).astype(jnp.float32) + bias[None]
        s_loc = s_loc.reshape(b, h, GRID_W, kr * GRID_W)
        s_ctx = jnp.einsum('bqhd,bchd->bhqc', q_i, k_ctx).astype(jnp.float32)
        p = jax.nn.softmax(jnp.concatenate([s_loc, s_ctx], axis=-1), axis=-1).astype(v.dtype)
        p_loc = p[..., :kr * GRID_W].reshape(b, h, GRID_W, kr, GRID_W)
        p_ctx = p[..., kr * GRID_W:]
        return (jnp.einsum('bhqrw,brwhd->bqhd', p_loc, v_i)
                + jnp.einsum('bhqc,bchd->bqhd', p_ctx, v_ctx))

    out = lax.map(row_block, jnp.arange(rows))
    return jnp.moveaxis(out, 0, 1).reshape(b, t, h * d)


def context_attention(q, k, v):
    b, n, h, d = q.shape
    s = jnp.einsum('bqhd,bkhd->bhqk', q, k).astype(jnp.float32)
    p = jax.nn.softmax(s, axis=-1).astype(v.dtype)
    return jnp.einsum('bhqk,bkhd->bqhd', p, v).reshape(b, n, h * d)


def centred_shift(p):
    padded = jnp.pad(p, ((0, 0), (1, 1), (0, 0)))
    return 0.5 * (padded[:, :-2] + padded[:, 2:])


def rwkv_prepare(p, shift_mu, decay_w0, decay_w2, iclr_a0, iclr_a2, gate_g2, key_k, key_a):
    b, n, _ = p.shape
    p = p + shift_mu * (centred_shift(p) - p)
    r, k, v, dw, da, dg = jnp.split(p, RWKV_SPLITS, axis=-1)
    heads = lambda t: t.astype(jnp.float32).reshape(b, n, RWKV_HEADS, RWKV_HEAD_DIM)
    g = jax.nn.sigmoid(dg) @ gate_g2
    kk = heads(k * key_k)
    kk = kk * lax.rsqrt(jnp.maximum(jnp.sum(kk * kk, axis=-1, keepdims=True), L2_EPS))
    per_dir = []
    for d in range(2):
        w_log = -jax.nn.softplus(-(decay_w0[d] + jnp.tanh(dw) @ decay_w2[d])) - 0.5
        a = jax.nn.sigmoid(iclr_a0[d] + da @ iclr_a2[d])
        k_d = k * (1 + (a - 1) * key_a)
        decay = jnp.exp(-jnp.exp(w_log.astype(jnp.float32)))
        per_dir.append((heads(decay), heads(k_d), heads(a)))
    return heads(r), heads(v), kk, g, per_dir


def rwkv_scan(state0, r, decay, k, v, kk, a, reverse):
    emit = r is not None

    def step(S, inp):
        r_t, w_t, k_t, v_t, kk_t, a_t = inp
        removal = jnp.einsum('bhvk,bhk->bhv', S, kk_t)
        S = (S * w_t[:, :, None, :]
             - removal[..., None] * (kk_t * a_t)[:, :, None, :]
             + v_t[..., None] * k_t[:, :, None, :])
        out = jnp.einsum('bhvk,bhk->bhv', S, r_t) if emit else None
        return S, out

    xs = jax.tree_util.tree_map(lambda t: jnp.moveaxis(t, 1, 0), (r, decay, k, v, kk, a))
    S, out = lax.scan(step, state0, xs, reverse=reverse)
    return S, (jnp.moveaxis(out, 0, 1) if emit else None)


def rwkv_readout(o_fwd, o_bwd, r, k_fwd, k_bwd, v, g, bonus_u, lnx_gain, lnx_bias):
    b, n, h, d = o_fwd.shape
    o = o_fwd + o_bwd
    mean = jnp.mean(o, axis=-1, keepdims=True)
    var = jnp.mean(jnp.square(o - mean), axis=-1, keepdims=True)
    o = ((o - mean) * lax.rsqrt(var + LNX_EPS)).reshape(b, n, h * d) * lnx_gain + lnx_bias
    bonus = jnp.sum(r * (k_fwd + k_bwd) * bonus_u, axis=-1, keepdims=True) * v
    return (o + bonus.reshape(b, n, h * d)) * g


def token_mixing(nx, ns, w_in, q_gain, k_gain, na_bias, shift_mu, decay_w0, decay_w2,
                 iclr_a0, iclr_a2, gate_g2, key_k, key_a, bonus_u, lnx_gain, lnx_bias,
                 w_out, emit_ctx):
    px = nx @ w_in
    ps = ns @ w_in
    q_x, k_x, v_x = attn_heads(px[..., :3 * ATTN_WIDTH], q_gain, k_gain)
    q_s, k_s, v_s = attn_heads(ps[..., :3 * ATTN_WIDTH], q_gain, k_gain)
    attn_x = neighbourhood_attention(q_x, k_x, v_x, k_s, v_s, na_bias)
    rwkv_args = (shift_mu, decay_w0, decay_w2, iclr_a0, iclr_a2, gate_g2, key_k, key_a)
    r_x, vr_x, kk_x, g_x, dirs_x = rwkv_prepare(px[..., 3 * ATTN_WIDTH:], *rwkv_args)
    r_s, vr_s, kk_s, g_s, dirs_s = rwkv_prepare(ps[..., 3 * ATTN_WIDTH:], *rwkv_args)
    state0 = jnp.zeros((ns.shape[0], RWKV_HEADS, RWKV_HEAD_DIM, RWKV_HEAD_DIM), jnp.float32)
    outs_x, outs_s = [], []
    for d in range(2):
        decay_s, kd_s, a_s = dirs_s[d]
        state_ctx, o_s = rwkv_scan(state0, r_s if emit_ctx else None, decay_s, kd_s, vr_s, kk_s, a_s, d == 1)
        decay_x, kd_x, a_x = dirs_x[d]
        _, o_x = rwkv_scan(state_ctx, r_x, decay_x, kd_x, vr_x, kk_x, a_x, d == 1)
        outs_x.append(o_x)
        outs_s.append(o_s)
    rwkv_x = rwkv_readout(outs_x[0], outs_x[1], r_x, dirs_x[0][1], dirs_x[1][1], vr_x, g_x,
                          bonus_u, lnx_gain, lnx_bias)
    y_x = jnp.concatenate([attn_x, rwkv_x.astype(attn_x.dtype)], axis=-1) @ w_out
    if not emit_ctx:
        return y_x, None
    attn_s = context_attention(q_s, k_s, v_s)
    rwkv_s = rwkv_readout(outs_s[0], outs_s[1], r_s, dirs_s[0][1], dirs_s[1][1], vr_s, g_s,
                          bonus_u, lnx_gain, lnx_bias)
    y_s = jnp.concatenate([attn_s, rwkv_s.astype(attn_s.dtype)], axis=-1) @ w_out
    return y_x, y_s


def setup_inputs(seed: int = 0) -> dict:
    key = jax.random.key(seed)
    ks = jax.random.split(key, 25)
    L, D, f32 = DEPTH, D_MODEL, jnp.float32

    def nrm(k, shape, scale):
        return scale * jax.random.normal(k, shape, f32)

    def uni(k, shape, lo, hi):
        return jax.random.uniform(k, shape, f32, lo, hi)

    return {
        'x': nrm(ks[0], (BATCH, SEQ, D), 1.0),
        'c': nrm(ks[1], (BATCH, D), 1.0),
        'ctx': nrm(ks[2], (BATCH, CTX_LEN, D), 1.0),
        'c_ctx': nrm(ks[3], (D,), 1.0),
        'w_mod': nrm(ks[4], (L, D, N_MOD * D), 0.5 * D ** -0.5),
        'b_mod': nrm(ks[5], (L, N_MOD * D), 0.02),
        'norm_gain': 1.0 + nrm(ks[6], (L, 3, D), 0.05),
        'ffn_up': nrm(ks[7], (L, 2, D, 2 * D_FF), D ** -0.5),
        'ffn_down': nrm(ks[8], (L, 2, D_FF, D), D_FF ** -0.5),
        'w_in': nrm(ks[9], (L, D, D_IN), D ** -0.5),
        'q_gain': 1.0 + nrm(ks[10], (L, ATTN_HEAD_DIM), 0.05),
        'k_gain': 1.0 + nrm(ks[11], (L, ATTN_HEAD_DIM), 0.05),
        'na_bias': nrm(ks[12], (L, ATTN_HEADS, 2 * NA_ROWS - 1, 2 * NA_COLS - 1), 0.1),
        'shift_mu': uni(ks[13], (L, RWKV_IN), 0.0, 1.0),
        'decay_w0': uni(ks[14], (L, 2, RWKV_WIDTH), -6.0, -1.0),
        'decay_w2': nrm(ks[15], (L, 2, DECAY_LORA, RWKV_WIDTH), 0.5 * DECAY_LORA ** -0.5),
        'iclr_a0': nrm(ks[16], (L, 2, RWKV_WIDTH), 0.1),
        'iclr_a2': nrm(ks[17], (L, 2, ICLR_LORA, RWKV_WIDTH), ICLR_LORA ** -0.5),
        'gate_g2': nrm(ks[18], (L, GATE_LORA, RWKV_WIDTH), GATE_LORA ** -0.5),
        'key_k': 0.85 + nrm(ks[19], (L, RWKV_WIDTH), 0.05),
        'key_a': 1.0 + nrm(ks[20], (L, RWKV_WIDTH), 0.05),
        'bonus_u': nrm(ks[21], (L, RWKV_HEADS, RWKV_HEAD_DIM), 0.1),
        'lnx_gain': 1.0 + nrm(ks[22], (L, RWKV_WIDTH), 0.05),
        'lnx_bias': nrm(ks[23], (L, RWKV_WIDTH), 0.02),
        'w_out': nrm(ks[24], (L, D, D), D ** -0.5),
    }


def reference(x, c, ctx, c_ctx, w_mod, b_mod, norm_gain, ffn_up, ffn_down, w_in, q_gain,
              k_gain, na_bias, shift_mu, decay_w0, decay_w2, iclr_a0, iclr_a2, gate_g2,
              key_k, key_a, bonus_u, lnx_gain, lnx_bias, w_out):
    s = ctx
    silu_c = jax.nn.silu(c)
    silu_cc = jax.nn.silu(c_ctx)[None, :]
    for l in range(DEPTH):
        last = l == DEPTH - 1
        mx = jnp.split((silu_c @ w_mod[l] + b_mod[l])[:, None, :], N_MOD, axis=-1)
        ms = jnp.split((silu_cc @ w_mod[l] + b_mod[l])[:, None, :], N_MOD, axis=-1)
        x = x + 0.5 * mx[2] * swiglu(modulate(rms_norm(x, norm_gain[l, 0]), mx[0], mx[1]),
                                     ffn_up[l, 0], ffn_down[l, 0])
        s = s + 0.5 * ms[2] * swiglu(modulate(rms_norm(s, norm_gain[l, 0]), ms[0], ms[1]),
                                     ffn_up[l, 0], ffn_down[l, 0])
        nx = modulate(rms_norm(x, norm_gain[l, 1]), mx[3], mx[4])
        ns = modulate(rms_norm(s, norm_gain[l, 1]), ms[3], ms[4])
        y_x, y_s = token_mixing(nx, ns, w_in[l], q_gain[l], k_gain[l], na_bias[l], shift_mu[l],
                                decay_w0[l], decay_w2[l], iclr_a0[l], iclr_a2[l], gate_g2[l],
                                key_k[l], key_a[l], bonus_u[l], lnx_gain[l], lnx_bias[l],
                                w_out[l], not last)
        x = x + mx[5] * y_x
        x = x + 0.5 * mx[8] * swiglu(modulate(rms_norm(x, norm_gain[l, 2]), mx[6], mx[7]),
                                     ffn_up[l, 1], ffn_down[l, 1])
        if not last:
            s = s + ms[5] * y_s
            s = s + 0.5 * ms[8] * swiglu(modulate(rms_norm(s, norm_gain[l, 2]), ms[6], ms[7]),
                                         ffn_up[l, 1], ffn_down[l, 1])
    return x
```

```python
import contextlib
import numpy as np
import ml_dtypes
import concourse.bass as bass
import concourse.mybir as mybir
from concourse.bass_utils import run_bass_kernel_spmd

F32 = mybir.dt.float32
BF16 = mybir.dt.bfloat16
AF = mybir.ActivationFunctionType
ALU = mybir.AluOpType
NPBF = ml_dtypes.bfloat16

D_MODEL = 1024
DEPTH = 4
BATCH = 4
SEQ = 4096
CTX = 256
LSEQ = SEQ + CTX
TOK = LSEQ // 2
D_FF = 2816
D_IN = 3328
NCORES = 8
RMS_EPS = 1e-6
TT = [(0, 512), (512, 512), (1024, 512), (1536, 512), (2048, 128)]
HALVES = [[0, 1], [2, 3, 4]]


def segs(c0, w):
    out = []
    if c0 < CTX:
        out.append((c0, min(c0 + w, CTX), 0))
    if c0 + w > CTX:
        out.append((max(c0, CTX), c0 + w, 1))
    return out


class Res:
    __slots__ = ("name", "w", "r")

    def __init__(self, name=""):
        self.name = name
        self.w = None
        self.r = []


class Ring:
    def __init__(self, tiles):
        self.tiles = tiles
        self.res = [Res() for _ in tiles]
        self.i = 0

    def next(self):
        i = self.i % len(self.tiles)
        self.i += 1
        return self.tiles[i], self.res[i], i


class Sched:
    ENGS = ("pe", "act", "dve", "pool", "sp")

    def __init__(self, nc, same_eng_sync=True):
        self.nc = nc
        self.ops = {e: [] for e in self.ENGS}
        self.dma_cnt = {}
        self.same = same_eng_sync
        self.known = {e: {} for e in self.ENGS}
        self.bar = {}

    def barrier(self):
        last = {}
        for e in self.ENGS:
            for i in range(len(self.ops[e]) - 1, -1, -1):
                if self.ops[e][i]["tok"][0] == "e":
                    last[e] = i
                    break
        for e in self.ENGS:
            need = self.bar.setdefault(e, {})
            for f, idx in last.items():
                if f != e:
                    need[("e", f)] = max(need.get(("e", f), -1), idx)
            for k, c in self.dma_cnt.items():
                need[("d", k)] = max(need.get(("d", k), -1), c)

    def op(self, eng, emit, reads=(), writes=(), dma_key=None):
        lst = self.ops[eng]
        idx = len(lst)
        need = {}

        def add(tok):
            if tok is None:
                return
            kind, src, val = tok
            if kind == "e" and src == eng and (eng == "pe" or not self.same):
                return
            if kind == "d":
                val = self.dma_cnt[src]
            k = (kind, src)
            if need.get(k, -1) < val:
                need[k] = val

        for r in reads:
            add(r.w)
        for w in writes:
            add(w.w)
            for t in w.r:
                add(t)
        for k, v in self.bar.pop(eng, {}).items():
            if need.get(k, -1) < v:
                need[k] = v
        kn = self.known[eng]
        waits = []
        for k, v in need.items():
            if kn.get(k, -1) >= v:
                continue
            kn[k] = v
            waits.append((k, v))
        if dma_key is not None:
            c = self.dma_cnt.get(dma_key, 0) + 1
            self.dma_cnt[dma_key] = c
            tok = ("d", dma_key, c)
        else:
            tok = ("e", eng, idx)
        lst.append({"emit": emit, "waits": waits, "tok": tok, "sig": False})
        for r in reads:
            r.r.append(tok)
        for w in writes:
            w.w = tok
            w.r = []
        return tok

    def mm(self, out, lhsT, rhs, start, stop, reads, writes):
        self.op("pe", lambda h: h.matmul(out, lhsT=lhsT, rhs=rhs, start=start, stop=stop), reads, writes)

    def transpose(self, out, in_, ident, reads, writes):
        self.op("pe", lambda h: h.transpose(out, in_, ident), reads, writes)

    def act(self, out, in_, func, reads, writes, bias=None, scale=None):
        kw = {}
        if bias is not None:
            kw["bias"] = bias
        if scale is not None:
            kw["scale"] = scale
        self.op("act", lambda h: h.activation(out=out, in_=in_, func=func, **kw), reads, writes)

    def tt(self, eng, out, in0, in1, op, reads, writes):
        self.op(eng, lambda h: h.tensor_tensor(out=out, in0=in0, in1=in1, op=op), reads, writes)

    def ts(self, eng, out, in0, s1, s2, op0, op1, reads, writes):
        if op1 is None:
            self.op(eng, lambda h: h.tensor_scalar(out=out, in0=in0, scalar1=s1, scalar2=None, op0=op0), reads, writes)
        else:
            self.op(eng, lambda h: h.tensor_scalar(out=out, in0=in0, scalar1=s1, scalar2=s2, op0=op0, op1=op1), reads, writes)

    def stt(self, eng, out, in0, scalar, in1, op0, op1, reads, writes):
        self.op(eng, lambda h: h.scalar_tensor_tensor(out=out, in0=in0, scalar=scalar, in1=in1, op0=op0, op1=op1), reads, writes)

    def recip(self, out, in_, reads, writes):
        self.op("dve", lambda h: h.reciprocal(out=out, in_=in_), reads, writes)

    def copy(self, eng, out, in_, reads, writes):
        self.op(eng, lambda h: h.tensor_copy(out=out, in_=in_), reads, writes)

    def memset(self, eng, ap, val, writes):
        self.op(eng, lambda h: h.memset(ap, val), (), writes)

    def dma(self, q, out, in_, reads, writes, key):
        self.op(q, lambda h: h.dma_start(out=out, in_=in_), reads, writes, dma_key=key)

    def emit_all(self, final_keys=()):
        nc = self.nc
        for e in self.ENGS:
            for o in self.ops[e]:
                for (kind, src), v in o["waits"]:
                    if kind == "e":
                        self.ops[src][v]["sig"] = True
        rank = {}
        for e in self.ENGS:
            c = 0
            rk = []
            for o in self.ops[e]:
                if o["sig"]:
                    c += 1
                rk.append(c)
            rank[e] = rk
        with contextlib.ExitStack() as st:
            esem = {e: st.enter_context(nc.semaphore("s_" + e)) for e in self.ENGS}
            dsem = {k: st.enter_context(nc.semaphore("d_" + str(k))) for k in self.dma_cnt}
            block = st.enter_context(nc.Block())

            def run(e, h):
                for o in self.ops[e]:
                    for (kind, src), v in o["waits"]:
                        if kind == "e":
                            h.wait_ge(esem[src], rank[src][v])
                        else:
                            h.wait_ge(dsem[src], 16 * v)
                    ins = o["emit"](h)
                    if o["tok"][0] == "d":
                        ins.then_inc(dsem[o["tok"][1]], 16)
                    elif o["sig"]:
                        ins.then_inc(esem[e], 1)
                if e == "sp":
                    for k in (self.dma_cnt if final_keys == "all" else final_keys):
                        if k in self.dma_cnt:
                            h.wait_ge(dsem[k], 16 * self.dma_cnt[k])

            @block.tensor
            def _(h):
                run("pe", h)

            @block.scalar
            def _(h):
                run("act", h)

            @block.vector
            def _(h):
                run("dve", h)

            @block.gpsimd
            def _(h):
                run("pool", h)

            @block.sync
            def _(h):
                run("sp", h)
        self.stats = {e: (len(self.ops[e]), rank[e][-1] if rank[e] else 0) for e in self.ENGS}


NMODCH = DEPTH * 9 * D_MODEL // 128


def emit_mod(nc, S, io, tag):
    with contextlib.ExitStack() as st:
        sb = lambda name, shape, dt=F32: st.enter_context(nc.sbuf_tensor(tag + name, shape, dt))
        c_sb = sb("c_sb", [128, 8, 2])
        sc_sb = sb("sc_sb", [128, 8, 2])
        b_sb = sb("b_sb", [128, NMODCH])
        o_sb = sb("o_sb", [128, 2, NMODCH])
        wring = Ring([sb(f"w{i}", [128, 8, 512]) for i in range(3)])
        psr = Ring([st.enter_context(nc.psum_tensor(tag + f"ps{i}", [128, 8], F32)) for i in range(4)])
        Rc, Rsc, Rb, Ro = Res(), Res(), Res(), Res()
        S.dma("sp", c_sb[:], io["cT"], (), [Rc], "c")
        S.dma("sp", b_sb[:], io["bm"], (), [Rb], "b")
        S.act(sc_sb[:], c_sb[:], AF.Silu, [Rc], [Rsc])
        for g in range(NMODCH // 4):
            l, col = (g * 4) // 72, ((g * 4) % 72) * 128
            w, rw, wi = wring.next()
            S.dma("sp", w[:], io["w_mod"][l].rearrange("(kc p) n -> p kc n", p=128)[:, :, col:col + 512], (), [rw], f"w{wi}")
            for jj in range(4):
                j = g * 4 + jj
                ps, rps, _ = psr.next()
                for k in range(8):
                    S.mm(ps[:, 0:2], w[:, k, jj * 128:(jj + 1) * 128], sc_sb[:, k, :], k == 0, k == 7, [rw, Rsc], [rps])
                S.act(o_sb[:, :, j], ps[:, 0:2], AF.Identity, [rps, Rb], [Ro], bias=b_sb[:, j:j + 1])
        S.dma("sp", io["mod_all"], o_sb[:], [Ro], (), "out")
        S.barrier()


SL_G5, SL_A2, SL_S2, SL_G8, SL_A0, SL_S0, SL_G2, SL_A1, SL_S1 = range(9)
NSL = 9


def emit_tl(nc, S, io, stageA, stageB, half, tag):
    xT, xT_o = io["xT_in"], io["xT_out"]
    if stageA:
        mixT, w_out, up2, dn2, ngP = io["mixT"], io["w_out"], io["up2"], io["dn2"], io["ngP"]
    if stageB:
        up1, dn1, w_in, qkg, ngC = io["up1"], io["dn1"], io["w_in"], io["qkg"], io["ngC"]
        qT_o, kT_o, v_o, rT_o = io["qT_o"], io["kT_o"], io["v_o"], io["rT_o"]
    mrows = (1, 0) if half == 0 else (0, 0)
    kc = lambda ap: ap.rearrange("(kc p) n -> p kc n", p=128)

    with contextlib.ExitStack() as st:
        sb = lambda name, shape, dt=F32: st.enter_context(nc.sbuf_tensor(tag + name, shape, dt))
        x_sb = sb("x_sb", [128, 8, TOK])
        nx_sb = sb("nx_sb", [128, 8, TOK], BF16)
        g_sb = sb("g_sb", [128, 22, 1152], BF16)
        wup_r = Ring([sb(f"wup{i}", [128, 8, 256], BF16) for i in range(2)])
        wup_res2 = [Res() for _ in range(2)]
        wdn_r = Ring([sb(f"wdn{i}", [128, 22, 128], BF16) for i in range(2)])
        sq_r = Ring([sb(f"sq{i}", [128, 512], BF16) for i in range(3)])
        rs_r = Ring([sb(f"rs{i}", [128, 512]) for i in range(2)])
        tmp_r = Ring([sb(f"tmp{i}", [128, 512]) for i in range(3)])
        sg_r = Ring([sb(f"sg{i}", [128, 512]) for i in range(2)])
        sc = sb("sc", [128, 2, NSL, 8])
        modP_sb = sb("modP_sb", [128, 2, 9, 8])
        modC_sb = sb("modC_sb", [128, 2, 9, 8])
        ngP_sb = sb("ngP_sb", [128, 3, 8])
        ngC_sb = sb("ngC_sb", [128, 3, 8])
        qkg_sb = sb("qkg_sb", [128, 2])
        qkg8 = sb("qkg8", [128, 2])
        ones_bf = sb("ones_bf", [128, 128], BF16)
        bd64 = sb("bd64", [128, 128], BF16)
        epsN = sb("epsN", [128, 1])
        epsQ = sb("epsQ", [128, 1])
        if stageB:
            wv_sb = sb("wv_sb", [128, 8, 512], BF16)
            stf_r = Ring([sb(f"stf{i}", [128, 512]) for i in range(2)])
            stb_r = Ring([sb(f"stb{i}", [128, 512], BF16) for i in range(2)])
        pst = [st.enter_context(nc.psum_tensor(tag + f"ps{i}", [128, 512], F32)) for i in range(8)]
        psg = Ring(pst[0:2])
        psu = Ring(pst[2:4])
        pso = Ring(pst[4:6])
        psn = Ring(pst[6:8])

        Rx = [[Res() for _ in TT] for _ in range(8)]
        Rnx = [[Res() for _ in TT] for _ in range(8)]
        Rg = [[Res() for _ in TT] for _ in range(22)]
        Rsc, Rconst, Rmod = Res(), Res(), Res()

        S.memset("dve", ones_bf[:], 1.0, [Rconst])
        S.memset("dve", bd64[:], 0.0, [Rconst])
        S.memset("dve", bd64[0:64, 0:64], 1.0, [Rconst])
        S.memset("dve", bd64[64:128, 64:128], 1.0, [Rconst])
        S.memset("dve", epsN[:], D_MODEL * RMS_EPS, [Rconst])
        S.memset("dve", epsQ[:], 64 * RMS_EPS, [Rconst])
        for k in range(8):
            for n, (c0, w) in enumerate(TT):
                S.dma("sp", x_sb[:, k, c0:c0 + w], xT[k * 128:(k + 1) * 128, c0:c0 + w], (), [Rx[k][n]], "xin")
        def ld_mod(dst, l):
            for ty in range(2):
                S.dma("sp", dst[:, ty], io["mod_all"][:, mrows[ty], l * 72:(l + 1) * 72].rearrange("p (m c) -> p m c", c=8),
                      (), [Rmod], "mods")
        if stageA:
            ld_mod(modP_sb, io["lA"])
            S.dma("sp", ngP_sb[:], ngP, (), [Rmod], "mods")
        if stageB:
            ld_mod(modC_sb, io["lB"])
            S.dma("sp", ngC_sb[:], ngC, (), [Rmod], "mods")
            S.dma("sp", qkg_sb[:], qkg, (), [Rmod], "mods")

        def mk_A(slot, mod_sb, ng_sb, gi, mi_scale):
            for ty in range(2):
                S.ts("dve", sc[:, ty, slot, :], mod_sb[:, ty, mi_scale, :], 1.0, 32.0, ALU.add, ALU.mult, [Rmod], [Rsc])
                S.tt("dve", sc[:, ty, slot, :], sc[:, ty, slot, :], ng_sb[:, gi, :], ALU.mult, [Rmod, Rsc], [Rsc])

        def mk_cp(slot, mod_sb, mi, mul):
            for ty in range(2):
                S.ts("dve", sc[:, ty, slot, :], mod_sb[:, ty, mi, :], mul, None, ALU.mult, None, [Rmod], [Rsc])

        if stageA:
            mk_cp(SL_G5, modP_sb, 5, 1.0)
            mk_A(SL_A2, modP_sb, ngP_sb, 2, 7)
            mk_cp(SL_S2, modP_sb, 6, 1.0)
            mk_cp(SL_G8, modP_sb, 8, 0.5)
        if stageB:
            mk_A(SL_A0, modC_sb, ngC_sb, 0, 1)
            mk_cp(SL_S0, modC_sb, 0, 1.0)
            mk_cp(SL_G2, modC_sb, 2, 0.5)
            mk_A(SL_A1, modC_sb, ngC_sb, 1, 4)
            mk_cp(SL_S1, modC_sb, 3, 1.0)
            S.ts("dve", qkg8[:, 0:1], qkg_sb[:, 0:1], 1.0, None, ALU.mult, None, [Rmod], [Rsc])
            S.ts("dve", qkg8[:, 1:2], qkg_sb[:, 1:2], 8.0, None, ALU.mult, None, [Rmod], [Rsc])

        def norm_mod(slA, slS):
            for n, (c0, w) in enumerate(TT):
                ps, rps, _ = psn.next()
                for k in range(8):
                    sq, rsq, _ = sq_r.next()
                    S.act(sq[:, :w], x_sb[:, k, c0:c0 + w], AF.Square, [Rx[k][n]], [rsq])
                    S.mm(ps[:, :w], ones_bf[:], sq[:, :w], k == 0, k == 7, [rsq, Rconst], [rps])
                rs, rrs, _ = rs_r.next()
                S.act(rs[:, :w], ps[:, :w], AF.Sqrt, [rps, Rconst], [rrs], bias=epsN[:, 0:1])
                S.recip(rs[:, :w], rs[:, :w], [rrs], [rrs])
                for k in range(8):
                    for (a, b, ty) in segs(c0, w):
                        tmp, rtmp, _ = tmp_r.next()
                        S.stt("dve", tmp[:, :b - a], x_sb[:, k, a:b], sc[:, ty, slA, k:k + 1], rs[:, a - c0:b - c0],
                              ALU.mult, ALU.mult, [Rx[k][n], rrs, Rsc], [rtmp])
                        S.ts("pool", nx_sb[:, k, a:b], tmp[:, :b - a], sc[:, ty, slS, k:k + 1], None, ALU.add, None,
                             [rtmp, Rsc], [Rnx[k][n]])

        def ffn(up_ap, dn_ap, slG):
            up_r = kc(up_ap)
            dn_r = kc(dn_ap)
            for half in HALVES:
                h0 = TT[half[0]][0]
                for j in range(22):
                    wu, rwu, wi = wup_r.next()
                    rwu2 = wup_res2[wi]
                    S.dma("pool", wu[:, :, 0:128], up_r[:, :, j * 128:(j + 1) * 128], (), [rwu], f"wup{wi}")
                    S.dma("pool", wu[:, :, 128:256], up_r[:, :, D_FF + j * 128:D_FF + (j + 1) * 128], (), [rwu2], f"wup{wi}")
                    for n in half:
                        c0, w = TT[n]
                        pg, rpg, _ = psg.next()
                        pu, rpu, _ = psu.next()
                        for k in range(8):
                            S.mm(pg[:, :w], wu[:, k, 0:128], nx_sb[:, k, c0:c0 + w], k == 0, k == 7, [rwu, Rnx[k][n]], [rpg])
                        for k in range(8):
                            S.mm(pu[:, :w], wu[:, k, 128:256], nx_sb[:, k, c0:c0 + w], k == 0, k == 7, [rwu2, Rnx[k][n]], [rpu])
                        sg, rsg, _ = sg_r.next()
                        S.act(sg[:, :w], pg[:, :w], AF.Silu, [rpg], [rsg])
                        S.tt("dve", g_sb[:, j, c0 - h0:c0 - h0 + w], sg[:, :w], pu[:, :w], ALU.mult, [rsg, rpu], [Rg[j][n]])
                for i in range(8):
                    wd, rwd, wi = wdn_r.next()
                    S.dma("pool", wd[:], dn_r[:, :, i * 128:(i + 1) * 128], (), [rwd], f"wdn{wi}")
                    for n in half:
                        c0, w = TT[n]
                        po, rpo, _ = pso.next()
                        for j in range(22):
                            S.mm(po[:, :w], wd[:, j, :], g_sb[:, j, c0 - h0:c0 - h0 + w], j == 0, j == 21, [rwd, Rg[j][n]], [rpo])
                        for (a, b, ty) in segs(c0, w):
                            S.stt("dve", x_sb[:, i, a:b], po[:, a - c0:b - c0], sc[:, ty, slG, i:i + 1], x_sb[:, i, a:b],
                                  ALU.mult, ALU.add, [rpo, Rx[i][n], Rsc], [Rx[i][n]])

        if stageA:
            for k in range(8):
                for n, (c0, w) in enumerate(TT):
                    S.dma("sp", nx_sb[:, k, c0:c0 + w], mixT[k * 128:(k + 1) * 128, c0:c0 + w], (), [Rnx[k][n]], "mixin")
            wo_r = kc(w_out)
            for i in range(8):
                wu, rwu, wi = wup_r.next()
                S.dma("pool", wu[:, :, 0:128], wo_r[:, :, i * 128:(i + 1) * 128], (), [rwu], f"wup{wi}")
                for n, (c0, w) in enumerate(TT):
                    po, rpo, _ = pso.next()
                    for k in range(8):
                        S.mm(po[:, :w], wu[:, k, 0:128], nx_sb[:, k, c0:c0 + w], k == 0, k == 7, [rwu, Rnx[k][n]], [rpo])
                    for (a, b, ty) in segs(c0, w):
                        S.stt("dve", x_sb[:, i, a:b], po[:, a - c0:b - c0], sc[:, ty, SL_G5, i:i + 1], x_sb[:, i, a:b],
                              ALU.mult, ALU.add, [rpo, Rx[i][n], Rsc], [Rx[i][n]])
            norm_mod(SL_A2, SL_S2)
            ffn(up2, dn2, SL_G8)

        if stageB:
            norm_mod(SL_A0, SL_S0)
            ffn(up1, dn1, SL_G2)

        for k in range(8):
            S.dma("sp", xT_o[k * 128:(k + 1) * 128, :], x_sb[:, k, :], [Rx[k][n] for n in range(len(TT))], (), "out")

        if stageB:
            norm_mod(SL_A1, SL_S1)
            wi_r = kc(w_in)
            Rwv = Res()
            S.dma("pool", wv_sb[:], wi_r[:, :, 1024:1536], (), [Rwv], "wv")
            chunks = [("q", c, c * 128) for c in range(4)] + [("k", c, 512 + c * 128) for c in range(4)] + \
                     [("r", c, 1536 + c * 128) for c in range(14)]
            for (kind, c, col) in chunks:
                wu, rwu, wi = wup_r.next()
                S.dma("pool", wu[:, :, 0:128], wi_r[:, :, col:col + 128], (), [rwu], f"wup{wi}")
                for n, (c0, w) in enumerate(TT):
                    po, rpo, _ = pso.next()
                    for k in range(8):
                        S.mm(po[:, :w], wu[:, k, 0:128], nx_sb[:, k, c0:c0 + w], k == 0, k == 7, [rwu, Rnx[k][n]], [rpo])
                    if kind == "r":
                        stf, rstf, si = stf_r.next()
                        S.act(stf[:, :w], po[:, :w], AF.Identity, [rpo], [rstf])
                        S.dma("sp", rT_o[c * 128:(c + 1) * 128, c0:c0 + w], stf[:, :w], [rstf], (), f"stf{si}")
                    else:
                        sq, rsq, _ = sq_r.next()
                        S.act(sq[:, :w], po[:, :w], AF.Square, [rpo], [rsq])
                        ps, rps, _ = psn.next()
                        S.mm(ps[:, :w], bd64[:], sq[:, :w], True, True, [rsq, Rconst], [rps])
                        rs, rrs, _ = rs_r.next()
                        S.act(rs[:, :w], ps[:, :w], AF.Sqrt, [rps, Rconst], [rrs], bias=epsQ[:, 0:1])
                        S.recip(rs[:, :w], rs[:, :w], [rrs], [rrs])
                        stb, rstb, si = stb_r.next()
                        gi = 0 if kind == "q" else 1
                        S.stt("dve", stb[:, :w], po[:, :w], qkg8[:, gi:gi + 1], rs[:, :w], ALU.mult, ALU.mult,
                              [rpo, rrs, Rsc], [rstb])
                        dst = qT_o if kind == "q" else kT_o
                        S.dma("sp", dst[c * 128:(c + 1) * 128, c0:c0 + w], stb[:, :w], [rstb], (), f"stb{si}")
            for t in range(TOK // 128):
                n = min(t // 4, 4)
                po, rpo, _ = pso.next()
                for k in range(8):
                    S.mm(po[:, :], nx_sb[:, k, t * 128:(t + 1) * 128], wv_sb[:, k, :], k == 0, k == 7, [Rwv, Rnx[k][n]], [rpo])
                stb, rstb, si = stb_r.next()
                S.act(stb[:, :], po[:, :], AF.Identity, [rpo], [rstb])
                S.dma("sp", v_o[t * 128:(t + 1) * 128, :], stb[:, :], [rstb], (), f"stb{si}")
        S.barrier()


NCH = LSEQ // 64
NT128 = LSEQ // 128
BLOCKS = [(0, 256)] + [(256 + 512 * i, 512) for i in range(8)]
NEG = -30000.0


def emit_mx(nc, S, io, tag, do_attn=True, do_rwkv=True, debug=False):
    identD = io["ident"]
    if do_attn:
        qT, kT, vtok, nab, namask, attn_o = io["qT"], io["kT"], io["vtok"], io["nab"], io["namask"], io["attn_o"]
    with contextlib.ExitStack() as st0:
        sb0 = lambda name, shape, dt=F32: st0.enter_context(nc.sbuf_tensor(tag + name, shape, dt))
        ident_f = sb0("ident_f", [128, 128])
        ident_b = sb0("ident_b", [128, 128], BF16)
        Rid = Res()
        S.dma("sp", ident_f[:], identD, (), [Rid], "c0")
        S.copy("dve", ident_b[:], ident_f[:], [Rid], [Rid])
        final_keys = []

        if do_attn:
            with contextlib.ExitStack() as st:
                sb = lambda name, shape, dt=F32: st.enter_context(nc.sbuf_tensor(tag + name, shape, dt))
                q_sb = sb("q_sb", [64, 4, LSEQ], BF16)
                k_sb = sb("k_sb", [64, 4, LSEQ], BF16)
                v_e = sb("v_e", [128, NT128, 4, 80], BF16)
                v_od = sb("v_od", [128, NT128 - 1, 4, 80], BF16)
                tu = sb("tu", [128, 4, 14, 64])
                msk = sb("msk", [128, 64])
                at_sb = sb("at_sb", [128, 2, LSEQ], BF16)
                sbias_r = Ring([sb(f"sbias{i}", [128, 256]) for i in range(2)])
                p_r = Ring([sb(f"p{i}", [128, 4, 64], BF16) for i in range(12)])
                rc_r = Ring([sb(f"rc{i}", [64, 4, 1]) for i in range(2)])
                on_r = Ring([sb(f"on{i}", [64, 256], BF16) for i in range(2)])
                pss_r = Ring([st.enter_context(nc.psum_tensor(tag + f"pss{i}", [128, 512], F32))[:, 0:256] for i in range(3)])
                pso_r = Ring([st.enter_context(nc.psum_tensor(tag + f"pso{i}", [128, 512], F32))[0:64, 0:264].rearrange("p (h e) -> p h e", e=66) for i in range(2)])
                pst_r = Ring([st.enter_context(nc.psum_tensor(tag + f"pst{i}", [128, 1024], BF16))[:, 0:64] for i in range(2)])
                Rq, Rk, Rve, Rvo, Rtu = Res(), Res(), Res(), Res(), Res()
                Rat_rows = []
                for hl in range(4):
                    S.dma("sp", q_sb[:, hl, :], qT[hl * 64:(hl + 1) * 64, :], (), [Rq], "aq")
                    S.dma("sp", k_sb[:, hl, :], kT[hl * 64:(hl + 1) * 64, :], (), [Rk], "ak")
                S.memset("dve", v_e[:, :, :, 64:80], 0.0, [Rve])
                S.memset("dve", v_e[:, :, :, 64:65], 1.0, [Rve])
                S.memset("dve", v_od[:, :, :, 64:80], 0.0, [Rvo])
                S.memset("dve", v_od[:, :, :, 64:65], 1.0, [Rvo])
                ve_src = vtok.rearrange("(n p) (h d) -> p n h d", p=128, h=4)
                vo_src = vtok[64:64 + 128 * (NT128 - 1), :].rearrange("(n p) (h d) -> p n h d", p=128, h=4)
                for hl in range(4):
                    for n0 in range(0, NT128, 17):
                        n1 = min(n0 + 17, NT128)
                        S.dma("sp", v_e[:, n0:n1, hl, 0:64], ve_src[:, n0:n1, hl, :], (), [Rve], "ave")
                        n1o = min(n0 + 17, NT128 - 1)
                        S.dma("sp", v_od[:, n0:n1o, hl, 0:64], vo_src[:, n0:n1o, hl, :], (), [Rvo], "avo")
                S.dma("sp", tu[:], nab, (), [Rtu], "atu")
                S.dma("sp", msk[:], namask, (), [Rtu], "atu")
                for hl in range(4):
                    S.tt("dve", tu[:, hl], tu[:, hl], msk[:].unsqueeze(1).to_broadcast([128, 14, 64]), ALU.add, [Rtu], [Rtu])

                import os
                lvl = int(os.environ.get("MXLVL", "9"))

                def qrow(t0, ktiles):
                    if lvl < 1:
                        return
                    ptl = []
                    for (koff, vt_ap, d0) in ktiles:
                        ps, rps, _ = pss_r.next()
                        for hl in range(4):
                            S.mm(ps[:, hl * 64:(hl + 1) * 64], k_sb[:, hl, koff:koff + 128],
                                 q_sb[:, hl, t0:t0 + 64], True, True, [Rq, Rk], [rps])
                        p, rp, _ = p_r.next()
                        if d0 is None:
                            S.act(p[:], ps.rearrange("p (h q) -> p h q", h=4), AF.Exp, [rps], [rp])
                        else:
                            sbt, rsb, _ = sbias_r.next()
                            S.tt("dve", sbt[:].rearrange("p (h q) -> p h q", h=4), ps.rearrange("p (h q) -> p h q", h=4),
                                 tu[:, :, d0, :], ALU.add, [rps, Rtu], [rsb])
                            S.act(p[:], sbt[:].rearrange("p (h q) -> p h q", h=4), AF.Exp, [rsb], [rp])
                        ptl.append((p, rp, vt_ap))
                    if lvl < 2:
                        return
                    po, rpo, _ = pso_r.next()
                    for hl in range(4):
                        for i, (p, rp, vt_ap) in enumerate(ptl):
                            S.mm(po[:, hl, :], p[:, hl, :], vt_ap[:, hl, 0:66], i == 0, i == len(ptl) - 1, [rp, Rve, Rvo], [rpo])
                    if lvl < 3:
                        return
                    rc, rrc, _ = rc_r.next()
                    S.recip(rc[:], po[:, :, 64:65], [rpo], [rrc])
                    on, ron, _ = on_r.next()
                    S.tt("dve", on[:].rearrange("p (h d) -> p h d", h=4), po[:, :, 0:64], rc[:].to_broadcast([64, 4, 64]),
                         ALU.mult, [rpo, rrc], [ron])
                    if lvl < 4:
                        return
                    for c2 in range(2):
                        pt, rpt, _ = pst_r.next()
                        S.transpose(pt[:, :], on[:, c2 * 128:(c2 + 1) * 128], ident_b[0:64, 0:64], [ron, Rid], [rpt])
                        rr = Res()
                        Rat_rows.append(rr)
                        S.act(at_sb[:, c2, t0:t0 + 64], pt[:, :], AF.Identity, [rpt], [rr])

                ctx_tiles = [(0, v_e[:, 0], None), (128, v_e[:, 1], None)]
                for qi in range(4):
                    qrow(qi * 64, ctx_tiles)
                for i in range(64 if lvl >= 9 else int(os.environ.get('MXROWS', '1'))):
                    start = min(max(i - 4, 0), 56)
                    tiles = []
                    for m in range(4):
                        r = start + 2 * m
                        vt_ap = v_e[:, 2 + r // 2] if r % 2 == 0 else v_od[:, (3 + r) // 2]
                        tiles.append((CTX + 64 * r, vt_ap, r - i + 7))
                    qrow(CTX + 64 * i, tiles + ctx_tiles)
                for c2 in range(2):
                    S.dma("sp", attn_o[c2 * 128:(c2 + 1) * 128, :], at_sb[:, c2, :], Rat_rows, (), "aout")
                S.barrier()

        if do_rwkv:
            L = dict(io)
            L.update({"ident_b": ident_b, "ident_f": ident_f, "Rid": Rid, "tag": tag})
            _mx_rwkv(nc, S, st0, L, final_keys, debug)
        S.barrier()


def _na_tables(na_bias_l, g):
    wk = np.arange(64)[:, None]
    wq = np.arange(64)[None, :]
    dc = np.clip(wk - wq, -15, 15) + 15
    cs = np.clip(np.arange(64) - 8, 0, 48)[None, :]
    inwin = (wk >= cs) & (wk < cs + 16)
    tab = np.empty((2, 64, 4, 14, 64), np.float32)
    for rl in range(2):
        for hl in range(4):
            for d0 in range(14):
                tab[rl, :, hl, d0, :] = na_bias_l[g * 4 + hl, d0 + rl][dc]
    mask = np.where(inwin, 0.0, NEG).astype(np.float32)
    return np.ascontiguousarray(tab.reshape(128, 4, 14, 64)), np.ascontiguousarray(np.concatenate([mask, mask], 0))


LNX_EPS = 64e-5
TRI_SEL = [(0, 1, 3), (2, 3, 1)]
AX = mybir.AxisListType


def _mx_rwkv(nc, S, st0, L, final_keys, debug):
    rrow, mu, pp, w0row, w2, a2, g2, lnrow, tri, amask, lmask, hsel, rw_o = (
        L[k] for k in ("rrow", "mu", "pp", "w0row", "w2", "a2", "g2", "lnrow", "tri", "amask", "lmask", "hsel", "rw_o"))
    tag = L["tag"]
    ident_b, ident_f, Rid = L["ident_b"], L["ident_f"], L["Rid"]
    with contextlib.ExitStack() as st:
        sb = lambda name, shape, dt=F32: st.enter_context(nc.sbuf_tensor(tag + name, shape, dt))
        ps_t = [st.enter_context(nc.psum_tensor(tag + f"rps{i}", [128, 512], F32)) for i in range(8)]
        rs = sb("rs", [128, 2, LSEQ], BF16)
        ks = sb("ks", [128, 2, LSEQ], BF16)
        kk = sb("kk", [128, 2, LSEQ], BF16)
        vt = sb("vt", [128, NT128, 256], BF16)
        lora6 = sb("lora6", [128, LSEQ], BF16)
        sdg = sb("sdg", [128, LSEQ], BF16)
        mu_sb = sb("mu_sb", [128, 8]); om_sb = sb("om_sb", [128, 8]); hm_sb = sb("hm_sb", [128, 8])
        pp_sb = sb("pp_sb", [128, 2, 8]); omka = sb("omka", [128, 2]); omka2 = sb("omka2", [128, 2])
        w0_f = sb("w0_f", [1, 2, 256]); w0_b = sb("w0_b", [1, 2, 256], BF16)
        ones_row = sb("ones_row", [1, 128], BF16)
        w2z = sb("w2z", [128, 2, 256], BF16)
        a2z = sb("a2z", [128, 2, 256], BF16)
        g2_sb = sb("g2_sb", [128, 256], BF16)
        lng = sb("lng", [128, 256]); lnb = sb("lnb", [128, 256])
        tri_sb = sb("tri_sb", [128, 4, 128])
        amask_sb = sb("amask_sb", [128, 2, 256]); lmask_sb = sb("lmask_sb", [128, 2, 64])
        hsel_f = sb("hsel_f", [128, 2]); hsel_b = sb("hsel_b", [128, 2], BF16)
        eps24 = sb("eps24", [128, 1]); epsln = sb("epsln", [128, 1])
        Rp = Res()
        KEY_K, KEY_A, BON_U, A0F, A0B = 0, 1, 2, 3, 4
        S.dma("sp", mu_sb[:], mu, (), [Rp], "rp")
        S.dma("sp", pp_sb[:], pp, (), [Rp], "rp")
        S.dma("sp", w0_f[:], w0row, (), [Rp], "rp")
        S.dma("sp", lng[:], lnrow[:, 0, :].partition_broadcast(128), (), [Rp], "rp")
        S.dma("sp", lnb[:], lnrow[:, 1, :].partition_broadcast(128), (), [Rp], "rp")
        S.dma("sp", tri_sb[:], tri, (), [Rp], "rp")
        S.dma("sp", amask_sb[:], amask, (), [Rp], "rp")
        S.dma("sp", lmask_sb[:], lmask, (), [Rp], "rp")
        S.dma("sp", hsel_f[:], hsel, (), [Rp], "rp")
        S.memset("dve", w2z[:], 0.0, [Rp])
        S.memset("dve", a2z[:], 0.0, [Rp])
        S.dma("pool", w2z[0:64], w2, [Rp], [Rp], "rpc")
        S.dma("pool", a2z[64:128], a2, [Rp], [Rp], "rpc")
        S.dma("pool", g2_sb[:], g2, (), [Rp], "rpc")
        S.memset("dve", ones_row[:], 1.0, [Rp])
        S.memset("dve", eps24[:], 1e-24, [Rp])
        S.memset("dve", epsln[:], LNX_EPS, [Rp])
        S.copy("dve", w0_b[:], w0_f[:], [Rp], [Rp])
        S.copy("dve", hsel_b[:], hsel_f[:], [Rp], [Rp])
        S.ts("dve", om_sb[:], mu_sb[:], -1.0, 1.0, ALU.mult, ALU.add, [Rp], [Rp])
        S.ts("dve", hm_sb[:], mu_sb[:], 0.5, None, ALU.mult, None, [Rp], [Rp])
        S.ts("dve", omka[:], pp_sb[:, :, KEY_A], -1.0, 1.0, ALU.mult, ALU.add, [Rp], [Rp])
        S.ts("dve", omka2[:], omka[:], 2.0, None, ALU.mult, None, [Rp], [Rp])
        na0 = sb("na0", [128, 2, 2])
        S.ts("dve", na0[:], pp_sb[:, :, A0F:A0F + 2], -1.0, None, ALU.mult, None, [Rp], [Rp])

        def sig_finish(out_ap, e_ap, rd, wr):
            S.ts("pool", e_ap, e_ap, 1.0, None, ALU.add, None, rd, rd)
            S.recip(e_ap, e_ap, rd, rd)
            if out_ap is not e_ap:
                S.copy("pool", out_ap, e_ap, rd, wr)
        bd64 = sb("bd64r", [128, 128], BF16)
        S.memset("dve", bd64[:], 0.0, [Rp])
        S.memset("dve", bd64[0:64, 0:64], 1.0, [Rp])
        S.memset("dve", bd64[64:128, 64:128], 1.0, [Rp])
        SEG = [(0, CTX, 0), (CTX + 1, LSEQ + 1, CTX)]

        import os
        rlvl = int(os.environ.get("RWLVL", "9"))
        with contextlib.ExitStack() as s1:
            sb1 = lambda name, shape, dt=F32: s1.enter_context(nc.sbuf_tensor(tag + name, shape, dt))
            raw_r = Ring([sb1(f"raw{i}", [128, LSEQ + 3]) for i in range(2)])
            tmp_r = Ring([sb1(f"stmp{i}", [128, LSEQ + 1]) for i in range(1)])
            t1_r = Ring([sb1(f"st1{i}", [128, LSEQ + 1]) for i in range(2)])
            vsf_r = Ring([sb1(f"vsf{i}", [128, LSEQ], BF16) for i in range(1)])
            ptr = Ring([ps_t[i][:, 0:64].bitcast(BF16) if False else ps_t[i] for i in range(6, 8)])
            for i in range(2):
                for c in (0, CTX + 1, LSEQ + 2):
                    S.memset("dve", raw_r.tiles[i][:, c:c + 1], 0.0, [raw_r.res[i]])
            Rrs, Rks, Rkk, Rvt, Rl6, Rsdg = Res(), Res(), Res(), Res(), Res(), Res()
            for fc in range(8):
                raw, rraw, ri = raw_r.next()
                S.dma("sp", raw[:, 1:CTX + 1], rrow(fc)[:, 0:CTX], (), [rraw], f"raw{ri}")
                S.dma("sp", raw[:, CTX + 2:LSEQ + 2], rrow(fc)[:, CTX:LSEQ], (), [rraw], f"raw{ri}")
                tmp, rtmp, _ = tmp_r.next()
                t1, rt1, _ = t1_r.next()
                S.tt("pool", tmp[:, :], raw[:, 0:LSEQ + 1], raw[:, 2:LSEQ + 3], ALU.add, [rraw], [rtmp])
                S.act(t1[:, :], raw[:, 1:LSEQ + 2], AF.Identity, [rraw, Rp], [rt1], scale=om_sb[:, fc:fc + 1])
                if fc < 4:
                    dst, rd = (rs, Rrs) if fc < 2 else (ks, Rks)
                    for (a, b, tk) in SEG:
                        S.stt("dve", dst[:, fc % 2, tk:tk + b - a], tmp[:, a:b], hm_sb[:, fc:fc + 1], t1[:, a:b],
                              ALU.mult, ALU.add, [rtmp, rt1, Rp], [rd])
                elif fc < 6:
                    vsf, rvsf, _ = vsf_r.next()
                    for (a, b, tk) in SEG:
                        S.stt("dve", vsf[:, tk:tk + b - a], tmp[:, a:b], hm_sb[:, fc:fc + 1], t1[:, a:b],
                              ALU.mult, ALU.add, [rtmp, rt1, Rp], [rvsf])
                    for n in range(NT128):
                        pt, rpt, _ = ptr.next()
                        ptb = pt[:, 0:64].bitcast(BF16)
                        S.transpose(ptb, vsf[:, n * 128:(n + 1) * 128], ident_b[:], [rvsf, Rid], [rpt])
                        if n % 2 == 0:
                            S.act(vt[:, n, (fc - 4) * 128:(fc - 3) * 128], ptb, AF.Identity, [rpt], [Rvt])
                        else:
                            S.copy("dve", vt[:, n, (fc - 4) * 128:(fc - 3) * 128], ptb, [rpt], [Rvt])
                else:
                    for (a, b, tk) in SEG:
                        S.stt("dve", t1[:, a:b], tmp[:, a:b], hm_sb[:, fc:fc + 1], t1[:, a:b],
                              ALU.mult, ALU.add, [rtmp, rt1, Rp], [rt1])
                    for (a, b, tk) in SEG:
                        if fc == 6:
                            S.copy("pool", lora6[64:128, tk:tk + b - a], t1[64:128, a:b], [rt1], [Rl6])
                            S.act(t1[0:64, a:b], t1[0:64, a:b], AF.Exp, [rt1], [rt1], scale=-2.0)
                            S.ts("pool", t1[0:64, a:b], t1[0:64, a:b], 1.0, None, ALU.add, None, [rt1], [rt1])
                            S.recip(t1[0:64, a:b], t1[0:64, a:b], [rt1], [rt1])
                            S.ts("dve", lora6[0:64, tk:tk + b - a], t1[0:64, a:b], 2.0, -1.0, ALU.mult, ALU.add, [rt1], [Rl6])
                        else:
                            S.act(t1[:, a:b], t1[:, a:b], AF.Exp, [rt1], [rt1], scale=-1.0)
                            sig_finish(sdg[:, tk:tk + b - a], t1[:, a:b], [rt1], [Rsdg])
            sq_r = Ring([sb1(f"ksq{i}", [128, 512], BF16) for i in range(2)])
            rt_r = Ring([sb1(f"krt{i}", [128, 512]) for i in range(2)])
            psk = Ring(ps_t[4:6])
            for fc2 in range(2 if rlvl >= 1 else 0):
                for (t0, W) in BLOCKS:
                    sq, rsq, _ = sq_r.next()
                    S.act(sq[:, :W], ks[:, fc2, t0:t0 + W], AF.Square, [Rks, Rp], [rsq], scale=pp_sb[:, fc2, KEY_K:KEY_K + 1])
                    ps, rps, _ = psk.next()
                    S.mm(ps[:, :W], bd64[:], sq[:, :W], True, True, [rsq, Rp], [rps])
                    rt, rrt, _ = rt_r.next()
                    S.act(rt[:, :W], ps[:, :W], AF.Sqrt, [rps, Rp], [rrt], bias=eps24[:, 0:1])
                    S.recip(rt[:, :W], rt[:, :W], [rrt], [rrt])
                    S.stt("dve", kk[:, fc2, t0:t0 + W], ks[:, fc2, t0:t0 + W], pp_sb[:, fc2, KEY_K:KEY_K + 1], rt[:, :W],
                          ALU.mult, ALU.mult, [Rks, rrt, Rp], [Rkk])

        S.barrier()
        with contextlib.ExitStack() as s2:
            sb2 = lambda name, shape, dt=F32: s2.enter_context(nc.sbuf_tensor(tag + name, shape, dt))
            o_acc = sb2("o_acc", [128, NT128, 256])
            rwst_r = Ring([sb2(f"rwst{i}", [128, 128], BF16) for i in range(4)])
            a_sb = sb2("a_sb", [128, 2, 512]); kd = sb2("kd", [128, 2, 512]); bb = sb2("bb", [128, 2, 512])
            tmpb = sb2("tmpb", [128, 512])
            sig_tok = sb2("sig_tok", [128, 4, 256])
            E1 = sb2("E1", [128, 2, 512]); E2 = sb2("E2", [128, 512]); E3 = sb2("E3", [128, 512]); E4 = sb2("E4", [128, 512])
            QRz = [sb2(f"QRz{i}", [128, 2, 8, 128], BF16) for i in range(2)]
            kt = sb2("kt", [128, 2, 512], BF16); bt = sb2("bt", [128, 2, 512], BF16)
            khat = sb2("khat", [128, 512], BF16); bhat = sb2("bhat", [128, 512], BF16)
            khat_tok = [sb2(f"khat_tok{i}", [128, 4, 256], BF16) for i in range(2)]
            bhat_tok = [sb2(f"bhat_tok{i}", [128, 4, 256], BF16) for i in range(2)]
            AT = [sb2(f"AT{i}", [128, 4, 256], BF16) for i in range(2)]
            Lt = [sb2(f"Lt{i}", [128, 4, 64], BF16) for i in range(2)]
            RLs = [[sb2(f"RLs{i}_{lv}", [128, 2, 4, 64], BF16) for lv in range(2)] for i in range(2)]
            Nt = [[sb2(f"Nt{i}_{j}", [128, 4, 64], BF16) for j in range(2)] for i in range(2)]
            Xs = [sb2(f"Xs{i}", [128, 4, 64], BF16) for i in range(2)]
            Us = [sb2(f"Us{i}", [128, 4, 64], BF16) for i in range(2)]
            H = sb2("H", [128, 2, 64]); Hb = sb2("Hb", [128, 2, 64], BF16)
            Rz = Res()
            for t in QRz + khat_tok + bhat_tok + AT + Lt + Xs + Us + [x for y in RLs for x in y] + [x for y in Nt for x in y]:
                S.memset("dve", t[:], 0.0, [Rz])
            RQR = [Res(), Res()]; Rkt, Rbt = Res(), Res()
            Rkht = [Res(), Res()]; Rbht = [Res(), Res()]
            RAT = [Res(), Res()]; RLt = [Res(), Res()]; RRL = [[Res() for _ in range(2)] for _ in range(2)]
            RN = [[Res(), Res()] for _ in range(2)]; RX = [Res(), Res()]; RU = [Res(), Res()]
            RH, RHb, RE1, Ra, Rkd, Rbb, Rsig, Roacc = Res(), Res(), Res(), Res(), Res(), Res(), Res(), Res()
            Ro = [Res() for _ in range(NT128)]
            pA0, pA1, pLN, pSQ, pXU, pOH, pB0, pB1 = ps_t
            RpA0, RpA1, RpLN, RpSQ, RpXU, RpOH, RpB0, RpB1 = [Res() for _ in range(8)]

            def v4(ap, rows, c0, n):
                return ap[rows[0]:rows[1], c0:c0 + 4 * n].rearrange("p (h e) -> p h e", h=4)

            for d in range(min(2, int(os.environ.get('RWND', '2'))) if rlvl >= 2 else 0):
                i1, i2, i3 = TRI_SEL[d]
                S.memset("dve", H[:], 0.0, [RH])
                S.memset("dve", Hb[:], 0.0, [RHb])
                border = list(range(9)) if d == 0 else [0] + list(range(8, 0, -1))
                border = border[:int(os.environ.get("RWNB", "9"))]
                for bi in border:
                    t0, W = BLOCKS[bi]
                    nck, ntl = W // 64, W // 128
                    for fc2 in range(2):
                        S.mm(pB0[:, :W], a2z[:, d, fc2 * 128:(fc2 + 1) * 128], lora6[:, t0:t0 + W], True, True, [Rp, Rl6], [RpB0])
                        S.act(a_sb[:, fc2, :W], pB0[:, :W], AF.Exp, [RpB0, Rp], [Ra], bias=na0[:, fc2, d:d + 1], scale=-1.0)
                        _e = a_sb[:, fc2, :W]
                        sig_finish(_e, _e, [Ra], [Ra])
                        S.ts("dve", tmpb[:, :W], a_sb[:, fc2, :W], pp_sb[:, fc2, KEY_A:KEY_A + 1], omka[:, fc2:fc2 + 1],
                             ALU.mult, ALU.add, [Ra, Rp], [Rkd])
                        S.tt("dve", kd[:, fc2, :W], ks[:, fc2, t0:t0 + W], tmpb[:, :W], ALU.mult, [Rks, Rkd], [Rkd])
                        S.tt("pool", bb[:, fc2, :W], kk[:, fc2, t0:t0 + W], a_sb[:, fc2, :W], ALU.mult, [Rkk, Ra], [Rbb])
                    for j in range(ntl):
                        S.mm(pB1[:, 0:256], lora6[:, t0 + j * 128:t0 + (j + 1) * 128], w2z[:, d, :], True, False, [Rp, Rl6], [RpB1])
                        S.mm(pB1[:, 0:256], ones_row[0:1, :], w0_b[0:1, d, :], False, True, [Rp], [RpB1])
                        S.act(sig_tok[:, j, :], pB1[:, 0:256], AF.Exp, [RpB1], [Rsig], scale=-1.0)
                        _e = sig_tok[:, j, :]
                        sig_finish(_e, _e, [Rsig], [Rsig])
                    for fc2 in range(2):
                        hb_rows = [(0, 64), (64, 128)]
                        for (bank, rb, ti) in ((pA0, RpA0, i1), (pA1, RpA1, i2), (pLN, RpLN, i3)):
                            for j in range(ntl):
                                S.mm(bank[:, j * 128:(j + 1) * 128], sig_tok[:, j, fc2 * 128:(fc2 + 1) * 128], tri_sb[:, ti, :],
                                     True, True, [Rsig, Rp], [rb])
                        S.act(E1[:, fc2, :W], pA0[:, :W], AF.Exp, [RpA0], [RE1])
                        S.act(E2[:, :W], pA0[:, :W], AF.Exp, [RpA0], [RE1], scale=-1.0)
                        S.act(E3[:, :W], pA1[:, :W], AF.Exp, [RpA1], [RE1])
                        S.act(E4[:, :W], pLN[:, :W], AF.Exp, [RpLN], [RE1])
                        c3 = lambda ap: ap.rearrange("p (c t) -> p c t", t=64)
                        for hp in range(2):
                            r0, r1 = hb_rows[hp]
                            S.tt("dve", QRz[hp][r0:r1, fc2, 0:nck, 64:128], c3(rs[r0:r1, fc2, t0:t0 + W]), c3(E1[r0:r1, fc2, :W]),
                                 ALU.mult, [Rrs, RE1], [RQR[hp]])
                            S.tt("pool", QRz[hp][r0:r1, fc2, 0:nck, 0:64], c3(kk[r0:r1, fc2, t0:t0 + W]), c3(E3[r0:r1, :W]),
                                 ALU.mult, [Rkk, RE1], [RQR[hp]])
                        S.tt("dve", kt[:, fc2, :W], kd[:, fc2, :W], E2[:, :W], ALU.mult, [Rkd, RE1], [Rkt])
                        S.tt("pool", bt[:, fc2, :W], bb[:, fc2, :W], E2[:, :W], ALU.mult, [Rbb, RE1], [Rbt])
                        S.tt("dve", khat[:, :W], kd[:, fc2, :W], E4[:, :W], ALU.mult, [Rkd, RE1], [Rkt])
                        S.stt("dve", bhat[:, :W], bb[:, fc2, :W], -1.0, E4[:, :W], ALU.mult, ALU.mult, [Rbb, RE1], [Rbt])
                        for j in range(ntl):
                            for (src, dstl, rl, bank, rb) in ((khat, khat_tok, Rkht, pB0, RpB0), (bhat, bhat_tok, Rbht, pB1, RpB1)):
                                ptb = bank[:, 0:64].bitcast(BF16)
                                S.transpose(ptb, src[:, j * 128:(j + 1) * 128], ident_b[:], [Rkt, Rbt, Rid], [rb])
                                S.act(dstl[0][0:64, j, fc2 * 128:(fc2 + 1) * 128], ptb[0:64, :], AF.Identity, [rb], [rl[0]])
                                S.copy("dve", dstl[1][64:128, j, fc2 * 128:(fc2 + 1) * 128], ptb[64:128, :], [rb], [rl[1]])
                    corder = list(range(nck)) if d == 0 else list(range(nck - 1, -1, -1))
                    BANKS = [((pA0, RpA0), (pA1, RpA1), (pLN, RpLN), (pSQ, RpSQ)),
                             ((pXU, RpXU), (pOH, RpOH), (pB0, RpB0), (pB1, RpB1))]

                    def stage_a(ci):
                        c = t0 // 64 + ci
                        pi = c % 2
                        rows = (pi * 64, pi * 64 + 64)
                        R0, R1 = rows
                        cs = slice(ci * 64, ci * 64 + 64)
                        (bA0, rA0), (bA1, rA1), (bLN, rLN), (bSQ, rSQ) = BANKS[pi]
                        for hl in range(4):
                            hp, fc2 = hl % 2, hl // 2
                            S.mm(bA0[R0:R1, hl * 128:(hl + 1) * 128], kt[:, fc2, cs], QRz[hp][:, fc2, ci, :], True, True,
                                 [Rkt, RQR[hp]], [rA0])
                            S.mm(bA1[R0:R1, hl * 128:(hl + 1) * 128], bt[:, fc2, cs], QRz[hp][:, fc2, ci, :], True, True,
                                 [Rbt, RQR[hp]], [rA1])
                            S.mm(bLN[R0:R1, hl * 64:(hl + 1) * 64], QRz[hp][:, fc2, ci, 0:64], bt[:, fc2, cs], True, True,
                                 [Rbt, RQR[hp]], [rLN])
                        S.tt("dve", AT[pi][R0:R1, :, 0:128], v4(bA0, rows, 0, 128),
                             amask_sb[R0:R1, d, 0:128].unsqueeze(1).to_broadcast([64, 4, 128]), ALU.mult, [rA0, Rp, Rz], [RAT[pi]])
                        S.tt("dve", AT[pi][R0:R1, :, 128:256], v4(bA1, rows, 0, 128),
                             amask_sb[R0:R1, d, 128:256].unsqueeze(1).to_broadcast([64, 4, 128]), ALU.mult, [rA1, Rp, Rz], [RAT[pi]])
                        S.tt("dve", Lt[pi][R0:R1, :, :], v4(bLN, rows, 0, 64),
                             lmask_sb[R0:R1, d, :].unsqueeze(1).to_broadcast([64, 4, 64]), ALU.mult, [rLN, Rp, Rz], [RLt[pi]])
                        S.tt("pool", Nt[pi][0][R0:R1, :, :], ident_f[R0:R1, R0:R1].unsqueeze(1).to_broadcast([64, 4, 64]),
                             AT[pi][R0:R1, :, 128:192], ALU.subtract, [RAT[pi], Rid, Rz], [RN[pi][0]])
                        return {"pi": pi, "rows": rows, "ci": ci, "c": c, "nidx": 0, "RL": None}

                    def inv_sq(stt_, lv):
                        pi, rows = stt_["pi"], stt_["rows"]
                        R0, R1 = rows
                        (bA0, rA0), (bA1, rA1), (bLN, rLN), (bSQ, rSQ) = BANKS[pi]
                        last = lv == 4
                        if stt_["RL"] is None:
                            curR = lambda hl: AT[pi][:, hl, 128:192]
                            curL = lambda hl: Lt[pi][:, hl, :]
                            rcur = [RAT[pi], RLt[pi]]
                        else:
                            tlp = stt_["RL"]
                            curR = lambda hl: tlp[:, 0, hl, :]
                            curL = lambda hl: tlp[:, 1, hl, :]
                            rcur = [stt_["RRL"]]
                        for hl in range(4):
                            if not last:
                                S.mm(bSQ[R0:R1, hl * 64:(hl + 1) * 64], curL(hl), curR(hl), True, True, rcur, [rSQ])
                            S.mm(bSQ[R0:R1, 256 + hl * 64:256 + (hl + 1) * 64], curR(hl), curL(hl), True, True, rcur, [rSQ])
                        tl = RLs[pi][lv % 2]
                        if not last:
                            S.act(tl[R0:R1, 0, :, :], v4(bSQ, rows, 0, 64), AF.Identity, [rSQ, Rz], [RRL[pi][lv % 2]])
                        S.copy("dve", tl[R0:R1, 1, :, :], v4(bSQ, rows, 256, 64), [rSQ, Rz], [RRL[pi][lv % 2]])
                        stt_["RL"], stt_["RRL"] = tl, RRL[pi][lv % 2]

                    def inv_prod(stt_, lv):
                        pi, rows = stt_["pi"], stt_["rows"]
                        R0, R1 = rows
                        (bA0, rA0), (bA1, rA1), (bLN, rLN), (bSQ, rSQ) = BANKS[pi]
                        tl, nidx = stt_["RL"], stt_["nidx"]
                        for hl in range(4):
                            S.mm(bLN[R0:R1, 256 + hl * 64:256 + (hl + 1) * 64], tl[:, 1, hl, :], Nt[pi][nidx][:, hl, :], True, True,
                                 [stt_["RRL"], RN[pi][nidx]], [rLN])
                        S.tt("dve", Nt[pi][1 - nidx][R0:R1, :, :], v4(bLN, rows, 256, 64), Nt[pi][nidx][R0:R1, :, :], ALU.add,
                             [rLN, RN[pi][nidx], Rz], [RN[pi][1 - nidx]])
                        stt_["nidx"] = 1 - nidx

                    def seq_stage(stt_):
                        pi, rows, ci, c = stt_["pi"], stt_["rows"], stt_["ci"], stt_["c"]
                        R0, R1 = rows
                        gt, jt = c // 2, ci // 2
                        Nf, RNf = Nt[pi][stt_["nidx"]], RN[pi][stt_["nidx"]]
                        for hl in range(4):
                            hp, fc2 = hl % 2, hl // 2
                            S.mm(pXU[R0:R1, hl * 64:(hl + 1) * 64], QRz[hp][:, fc2, ci, 0:64], Hb[:, fc2, :], True, False,
                                 [RQR[hp], RHb], [RpXU])
                            S.mm(pXU[R0:R1, hl * 64:(hl + 1) * 64], AT[pi][:, hl, 0:64], vt[:, gt, hl * 64:(hl + 1) * 64], False, True,
                                 [RAT[pi], Rvt], [RpXU])
                        S.act(Xs[pi][R0:R1, :, :], v4(pXU, rows, 0, 64), AF.Identity, [RpXU, Rz], [RX[pi]])
                        for hl in range(4):
                            S.mm(pXU[R0:R1, 256 + hl * 64:256 + (hl + 1) * 64], Nf[:, hl, :], Xs[pi][:, hl, :], True, True,
                                 [RNf, RX[pi]], [RpXU])
                        S.copy("dve", Us[pi][R0:R1, :, :], v4(pXU, rows, 256, 64), [RpXU, Rz], [RU[pi]])
                        for hl in range(4):
                            hp, fc2 = hl % 2, hl // 2
                            S.mm(pOH[R0:R1, hl * 64:(hl + 1) * 64], QRz[hp][:, fc2, ci, 64:128], Hb[:, fc2, :], True, False,
                                 [RQR[hp], RHb], [RpOH])
                            S.mm(pOH[R0:R1, hl * 64:(hl + 1) * 64], AT[pi][:, hl, 64:128], vt[:, gt, hl * 64:(hl + 1) * 64], False, False,
                                 [RAT[pi], Rvt], [RpOH])
                            S.mm(pOH[R0:R1, hl * 64:(hl + 1) * 64], AT[pi][:, hl, 192:256], Us[pi][:, hl, :], False, True,
                                 [RAT[pi], RU[pi]], [RpOH])
                        for hl in range(4):
                            hp, fc2 = hl % 2, hl // 2
                            S.mm(pB1[hp * 64:hp * 64 + 64, fc2 * 64:(fc2 + 1) * 64], khat_tok[pi][:, jt, hl * 64:(hl + 1) * 64],
                                 vt[:, gt, hl * 64:(hl + 1) * 64], True, False, [Rkht[pi], Rvt], [RpB1])
                            S.mm(pB1[hp * 64:hp * 64 + 64, fc2 * 64:(fc2 + 1) * 64], bhat_tok[pi][:, jt, hl * 64:(hl + 1) * 64],
                                 Us[pi][:, hl, :], False, True, [Rbht[pi], RU[pi]], [RpB1])
                        if d == 0:
                            S.act(o_acc[R0:R1, gt, :], pOH[R0:R1, 0:256], AF.Identity, [RpOH], [Ro[gt]])
                        else:
                            S.tt("dve", o_acc[R0:R1, gt, :], pOH[R0:R1, 0:256], o_acc[R0:R1, gt, :], ALU.add, [RpOH, Ro[gt]], [Ro[gt]])
                        pcc = ci * 64 + (63 if d == 0 else 0)
                        for fc2 in range(2):
                            S.ts("pool", H[:, fc2, :], H[:, fc2, :], E1[:, fc2, pcc:pcc + 1], None, ALU.mult, None, [RH, RE1], [RH])
                            S.tt("dve", H[:, fc2, :], pB1[:, fc2 * 64:(fc2 + 1) * 64], H[:, fc2, :], ALU.add, [RH, RpB1], [RH])
                        S.act(Hb[:], H[:], AF.Identity, [RH], [RHb])

                    for p0 in range(0, len(corder), 2):
                        grp = corder[p0:p0 + 2]
                        sts = [stage_a(ci) for ci in grp]
                        for lv in range(5):
                            for s_ in sts:
                                inv_sq(s_, lv)
                            for s_ in sts:
                                inv_prod(s_, lv)
                        for s_ in sts:
                            seq_stage(s_)
            if debug:
                S.dma("sp", L["o_dbg"], o_acc[:], Ro, (), "odbg")
                final_keys.append("odbg")
                dbg = {"dAT": (AT[0], BF16), "dN0": (Nt[0][0], BF16), "dN1": (Nt[0][1], BF16), "dQ": (QRz[0], BF16), "dK": (kt, BF16), "dB": (bt, BF16),
                       "dH": (H, F32), "dE1": (E1, F32), "dE2": (E2, F32), "dkd": (kd, F32), "dbb": (bb, F32), "da": (a_sb, F32),
                       "dsig": (sig_tok, F32), "dX": (Xs[0], BF16), "dU": (Us[0], BF16), "dkh": (khat_tok[0], BF16), "dL": (Lt[0], BF16)}
                allres = [Rz, RH, RE1, Ra, Rkd, Rbb, Rsig, Rkt, Rbt] + RQR + RAT + RLt + RX + RU + Rkht + Rbht + [x for y in RN for x in y]
                for nm, (t, dt_) in dbg.items():
                    dd = nc.dram_tensor(nm, list(t.shape), dt_, kind="ExternalOutput").ap()
                    S.dma("sp", dd, t[:], allres, (), "odbg")

            prod = kt
            a2s = tmpb
            bon = sb2("bon", [128, 4]); s1t = sb2("s1t", [128, 4]); s2t = sb2("s2t", [128, 4]); mean = sb2("mean", [128, 4])
            var = sb2("var", [128, 4]); sqt = E2[:, 0:256]; yt = E3[:, 0:256]; bv = E4[:, 0:256]
            yb = sb2("yb", [128, 256], BF16)
            Rprod, Rst, Ry, Ryb, Rrw = Res(), Res(), Res(), Res(), Res()
            for (t0, W) in (BLOCKS if rlvl >= 6 else []):
                for fc2 in range(2):
                    for d in range(2):
                        S.mm(pB0[:, :W], a2z[:, d, fc2 * 128:(fc2 + 1) * 128], lora6[:, t0:t0 + W], True, True, [Rp, Rl6], [RpB0])
                        adst = a_sb[:, 0, :W] if d == 0 else a2s[:, :W]
                        S.act(adst, pB0[:, :W], AF.Exp, [RpB0, Rp], [Ra], bias=na0[:, fc2, d:d + 1], scale=-1.0)
                        sig_finish(adst, adst, [Ra], [Ra])
                    S.tt("dve", a2s[:, :W], a2s[:, :W], a_sb[:, 0, :W], ALU.add, [Ra], [Ra])
                    S.ts("dve", a2s[:, :W], a2s[:, :W], pp_sb[:, fc2, KEY_A:KEY_A + 1], omka2[:, fc2:fc2 + 1], ALU.mult, ALU.add, [Ra, Rp], [Ra])
                    S.tt("dve", a2s[:, :W], a2s[:, :W], ks[:, fc2, t0:t0 + W], ALU.mult, [Ra, Rks], [Ra])
                    S.stt("dve", prod[:, fc2, :W], a2s[:, :W], pp_sb[:, fc2, BON_U:BON_U + 1], rs[:, fc2, t0:t0 + W], ALU.mult, ALU.mult,
                          [Ra, Rrs, Rp], [Rprod])
                for j in range(W // 128):
                    gt = t0 // 128 + j
                    tk = slice(t0 + j * 128, t0 + (j + 1) * 128)
                    for fc2 in range(2):
                        S.mm(pB1[:, 256 + fc2 * 2:256 + fc2 * 2 + 2], prod[:, fc2, j * 128:(j + 1) * 128], hsel_b[:], True, True, [Rprod, Rp], [RpB1])
                    S.mm(pB1[:, 0:256], sdg[:, tk], g2_sb[:], True, True, [Rsdg, Rp], [RpB1])
                    o3 = o_acc[:, gt, :].rearrange("p (h e) -> p h e", h=4)
                    S.copy("dve", bon[:], pB1[:, 256:260], [RpB1], [Rst])
                    S.op("dve", (lambda o3, s1t: (lambda h: h.reduce_sum(out=s1t[:], in_=o3, axis=AX.X)))(o3, s1t), [Ro[gt]], [Rst])
                    S.act(sqt, o_acc[:, gt, :], AF.Square, [Ro[gt]], [Rst])
                    S.op("dve", (lambda sq3, s2t: (lambda h: h.reduce_sum(out=s2t[:], in_=sq3, axis=AX.X)))(sqt.rearrange("p (h e) -> p h e", h=4), s2t), [Rst], [Rst])
                    S.ts("dve", mean[:], s1t[:], 1.0 / 64, None, ALU.mult, None, [Rst], [Rst])
                    S.tt("dve", var[:], mean[:], mean[:], ALU.mult, [Rst], [Rst])
                    S.stt("dve", var[:], s2t[:], 1.0 / 64, var[:], ALU.mult, ALU.subtract, [Rst], [Rst])
                    S.act(var[:], var[:], AF.Sqrt, [Rst, Rp], [Rst], bias=epsln[:, 0:1])
                    S.recip(var[:], var[:], [Rst], [Rst])
                    y3 = yt.rearrange("p (h e) -> p h e", h=4)
                    S.tt("dve", y3, o3, mean[:].unsqueeze(2).to_broadcast([128, 4, 64]), ALU.subtract, [Ro[gt], Rst], [Ry])
                    S.tt("dve", y3, y3, var[:].unsqueeze(2).to_broadcast([128, 4, 64]), ALU.mult, [Ry, Rst], [Ry])
                    S.tt("pool", yt, yt, lng[:], ALU.mult, [Ry, Rp], [Ry])
                    S.tt("pool", yt, yt, lnb[:], ALU.add, [Ry, Rp], [Ry])
                    S.tt("dve", bv.rearrange("p (h e) -> p h e", h=4), vt[:, gt, :].rearrange("p (h e) -> p h e", h=4),
                         bon[:].unsqueeze(2).to_broadcast([128, 4, 64]), ALU.mult, [Rvt, Rst], [Ry])
                    S.tt("pool", yt, yt, bv, ALU.add, [Ry], [Ry])
                    S.tt("dve", yb[:], yt, pB1[:, 0:256], ALU.mult, [Ry, RpB1], [Ryb])
                    for fc2 in range(2):
                        ptb = pB0[:, 0:64].bitcast(BF16)
                        S.transpose(ptb, yb[:, fc2 * 128:(fc2 + 1) * 128], ident_b[:], [Ryb, Rid], [RpB0])
                        rwst, rrwst, rwi = rwst_r.next()
                        S.act(rwst[:], ptb, AF.Identity, [RpB0], [rrwst])
                        S.dma("sp", rw_o[fc2 * 128:(fc2 + 1) * 128, tk], rwst[:], [rrwst], (), f"rwout{rwi}")
            S.barrier()


NACT = 4


def build_fused():
    nc = bass.Bass("TRN2", target_bir_lowering=False)
    din = lambda name, shape, dt=F32: nc.dram_tensor(name, shape, dt, kind="ExternalInput").ap()
    dout = lambda name, shape, dt=F32: nc.dram_tensor(name, shape, dt, kind="ExternalOutput").ap()
    dscr = lambda name, shape, dt=F32: nc.dram_tensor(name, shape, dt).ap()
    xin = [din(f"xT{h}", [D_MODEL, TOK]) for h in range(2)]
    xout = [dout(f"xo{h}", [D_MODEL, TOK]) for h in range(2)]
    xs = [dscr(f"xs{h}", [D_MODEL, TOK]) for h in range(2)]
    cT = din("cT", [128, 8, 2])
    bm = din("bm", [128, NMODCH])
    w_mod = din("w_mod", [DEPTH, D_MODEL, 9 * D_MODEL])
    ffn_up = din("ffn_up", [DEPTH, 2, D_MODEL, 2 * D_FF])
    ffn_down = din("ffn_down", [DEPTH, 2, D_FF, D_MODEL])
    w_in = din("w_in", [DEPTH, D_MODEL, D_IN])
    w_out = din("w_out", [DEPTH, D_MODEL, D_MODEL])
    ng = din("ng", [DEPTH, 128, 3, 8])
    qkg = din("qkg", [DEPTH, 128, 2])
    nab = din("nab", [DEPTH, 2, 128, 4 * 14 * 64])
    namask = din("namask", [128, 64])
    ident = din("ident", [128, 128])
    tri = din("tri", [128, 512])
    amask = din("amask", [128, 512])
    lmask = din("lmask", [128, 128])
    hsel = din("hsel", [128, 2])
    mu = din("mu", [DEPTH, 2, 128, 8])
    pp = din("pp", [DEPTH, 2, 128, 16])
    w0row = din("w0row", [DEPTH, 2, 1, 512])
    w2 = din("w2", [DEPTH, 2, 64, 512])
    a2 = din("a2", [DEPTH, 2, 64, 512])
    g2 = din("g2", [DEPTH, 2, 128, 256])
    lnrow = din("lnrow", [DEPTH, 2, 1, 512])
    mod_all = dscr("mod_all", [128, 2, NMODCH])
    qT_full = dscr("qT_full", [512, LSEQ], BF16)
    kT_full = dscr("kT_full", [512, LSEQ], BF16)
    v_full = dscr("v_full", [LSEQ, 512], BF16)
    rT_full = dscr("rT_full", [1792, LSEQ])
    mix_full = dscr("mix_full", [D_MODEL, LSEQ], BF16)

    S = Sched(nc)
    emit_mod(nc, S, {"cT": cT, "bm": bm, "w_mod": w_mod, "mod_all": mod_all}, "m_")
    r2 = lambda ap: ap.rearrange("p (a b) -> p a b", a=2)
    for l in range(DEPTH + 1):
        stageA, stageB = l > 0, l < DEPTH
        for half in range(2):
            hs = slice(half * TOK, (half + 1) * TOK)
            io = {"xT_in": xin[half] if l == 0 else xs[half], "xT_out": xout[half] if l == DEPTH else xs[half],
                  "mod_all": mod_all, "lA": l - 1, "lB": l}
            if stageA:
                io.update(mixT=mix_full[:, hs], w_out=w_out[l - 1], up2=ffn_up[l - 1, 1], dn2=ffn_down[l - 1, 1], ngP=ng[l - 1])
            if stageB:
                io.update(up1=ffn_up[l, 0], dn1=ffn_down[l, 0], w_in=w_in[l], qkg=qkg[l], ngC=ng[l],
                          qT_o=qT_full[:, hs], kT_o=kT_full[:, hs], v_o=v_full[hs, :], rT_o=rT_full[:, hs])
            emit_tl(nc, S, io, stageA, stageB, half, f"t{l}{half}_")
        if not stageB:
            break
        for g in range(2):
            def rrow(fc, g=g):
                if fc < 6:
                    base = (fc // 2) * 512 + g * 256 + (fc % 2) * 128
                else:
                    base = 1536 + (fc - 6) * 128
                return rT_full[base:base + 128, :]
            io = {"ident": ident, "qT": qT_full[g * 256:(g + 1) * 256, :], "kT": kT_full[g * 256:(g + 1) * 256, :],
                  "vtok": v_full[:, g * 256:(g + 1) * 256],
                  "nab": nab[l, g].rearrange("p (h d q) -> p h d q", h=4, d=14), "namask": namask,
                  "attn_o": mix_full[g * 256:(g + 1) * 256, :], "rw_o": mix_full[512 + g * 256:512 + (g + 1) * 256, :],
                  "rrow": rrow, "mu": mu[l, g], "pp": r2(pp[l, g]), "w0row": w0row[l, g].rearrange("o (a b) -> o a b", a=2),
                  "w2": r2(w2[l, g]), "a2": r2(a2[l, g]), "g2": g2[l, g], "lnrow": w0row[l, g].rearrange("o (a b) -> o a b", a=2) if False else lnrow[l, g].rearrange("o (a b) -> o a b", a=2),
                  "tri": tri.rearrange("p (a b) -> p a b", a=4), "amask": r2(amask), "lmask": r2(lmask), "hsel": hsel}
            emit_mx(nc, S, io, f"x{l}{g}_")
    S.emit_all(final_keys="all")
    nc._sched_stats = S.stats
    return nc


_PROG = {}


def _rw_params(P, g):
    gs = slice(g * 256, (g + 1) * 256)
    rows = np.concatenate([np.arange(512)[gs], 512 + np.arange(512)[gs], 1024 + np.arange(512)[gs], 1536 + np.arange(256)])
    pm = lambda v: np.ascontiguousarray(v.reshape(-1, 128).T)
    u = P['bonus_u'].reshape(512)
    pp = np.zeros((128, 2, 8), np.float32)
    for i, v in enumerate([P['key_k'][gs], P['key_a'][gs], u[gs], P['iclr_a0'][0][gs], P['iclr_a0'][1][gs]]):
        pp[:, :, i] = pm(v)
    return {
        "mu": pm(P['shift_mu'][rows]),
        "pp": pp.reshape(128, 16),
        "w0row": np.ascontiguousarray(P['decay_w0'][:, gs]).reshape(1, 512),
        "w2": np.ascontiguousarray(P['decay_w2'][:, :, gs].transpose(1, 0, 2)).reshape(64, 512),
        "a2": np.ascontiguousarray(P['iclr_a2'][:, :, gs].transpose(1, 0, 2)).reshape(64, 512),
        "g2": np.ascontiguousarray(P['gate_g2'][:, gs]),
        "lnrow": np.ascontiguousarray(np.stack([P['lnx_gain'][gs], P['lnx_bias'][gs]], 0)).reshape(1, 512),
    }


def _consts():
    e = float(np.exp(np.float32(-0.5)))
    s = np.arange(128)[:, None]; t = np.arange(128)[None, :]
    same = (s // 64) == (t // 64)
    tri = np.stack([same & (s <= t), same & (s < t), same & (s >= t), same & (s > t)], axis=1).astype(np.float32) * (-e)
    s6 = np.arange(64)[:, None]; t6 = np.arange(64)[None, :]
    am, lm = [], []
    for d in range(2):
        st_ = (s6 < t6) if d == 0 else (s6 > t6)
        le_ = (s6 <= t6) if d == 0 else (s6 >= t6)
        am.append(np.concatenate([st_, le_, st_, -1.0 * le_], axis=1).astype(np.float32))
        lm.append(((s6 > t6) if d == 0 else (s6 < t6)).astype(np.float32))
    amask = np.stack(am, axis=1).reshape(64, 512)
    lmask = np.stack(lm, axis=1).reshape(64, 128)
    hsel = np.zeros((128, 2), np.float32); hsel[:64, 0] = 1; hsel[64:, 1] = 1
    return {"tri": np.ascontiguousarray(tri.reshape(128, 512)), "amask": np.ascontiguousarray(np.concatenate([amask, amask], 0)),
            "lmask": np.ascontiguousarray(np.concatenate([lmask, lmask], 0)), "hsel": hsel,
            "ident": np.eye(128, dtype=np.float32)}


def kernel(x, c, ctx, c_ctx, w_mod, b_mod, norm_gain, ffn_up, ffn_down, w_in, q_gain, k_gain, na_bias,
           shift_mu, decay_w0, decay_w2, iclr_a0, iclr_a2, gate_g2, key_k, key_a, bonus_u, lnx_gain,
           lnx_bias, w_out):
    f = lambda a: np.ascontiguousarray(np.asarray(a, dtype=np.float32))
    x, c, ctx, c_ctx, w_mod, b_mod, norm_gain, ffn_up, ffn_down, w_in = map(f, (x, c, ctx, c_ctx, w_mod, b_mod, norm_gain, ffn_up, ffn_down, w_in))
    q_gain, k_gain, na_bias, shift_mu, decay_w0, decay_w2, iclr_a0, iclr_a2 = map(f, (q_gain, k_gain, na_bias, shift_mu, decay_w0, decay_w2, iclr_a0, iclr_a2))
    gate_g2, key_k, key_a, bonus_u, lnx_gain, lnx_bias, w_out = map(f, (gate_g2, key_k, key_a, bonus_u, lnx_gain, lnx_bias, w_out))
    shared = dict(_consts())
    shared.update({"w_mod": w_mod, "ffn_up": ffn_up, "ffn_down": ffn_down, "w_in": w_in, "w_out": w_out})
    shared["bm"] = np.ascontiguousarray(b_mod.reshape(NMODCH, 128).T)
    shared["ng"] = np.ascontiguousarray(norm_gain.reshape(DEPTH, 3, 8, 128).transpose(0, 3, 1, 2))
    shared["qkg"] = np.ascontiguousarray(np.stack([np.tile(q_gain, (1, 2)), np.tile(k_gain, (1, 2))], axis=2))
    nabs, mask = [], None
    for l in range(DEPTH):
        row = []
        for g in range(2):
            tab, mask = _na_tables(na_bias[l], g)
            row.append(tab.reshape(128, 4 * 14 * 64))
        nabs.append(np.stack(row, 0))
    shared["nab"] = np.ascontiguousarray(np.stack(nabs, 0))
    shared["namask"] = mask
    rw = {k: [] for k in ("mu", "pp", "w0row", "w2", "a2", "g2", "lnrow")}
    for l in range(DEPTH):
        P = {"shift_mu": shift_mu[l], "decay_w0": decay_w0[l], "decay_w2": decay_w2[l], "iclr_a0": iclr_a0[l],
             "iclr_a2": iclr_a2[l], "gate_g2": gate_g2[l], "key_k": key_k[l], "key_a": key_a[l], "bonus_u": bonus_u[l],
             "lnx_gain": lnx_gain[l], "lnx_bias": lnx_bias[l]}
        per_g = [_rw_params(P, g) for g in range(2)]
        for k in rw:
            rw[k].append(np.stack([per_g[0][k], per_g[1][k]], 0))
    for k in rw:
        shared[k] = np.ascontiguousarray(np.stack(rw[k], 0))
    seq = np.concatenate([ctx, x], axis=1)
    in_maps = []
    for b in range(NACT):
        m = dict(shared)
        for h in range(2):
            m[f"xT{h}"] = np.ascontiguousarray(seq[b, h * TOK:(h + 1) * TOK].T)
        rows = np.stack([c[b], c_ctx], axis=0)
        m["cT"] = np.ascontiguousarray(rows.reshape(2, 8, 128).transpose(2, 1, 0))
        in_maps.append(m)
    if "f" not in _PROG:
        _PROG["f"] = build_fused()
    res = run_bass_kernel_spmd(_PROG["f"], in_maps, core_ids=list(range(NACT))).results
    out = np.empty((BATCH, SEQ, D_MODEL), np.float32)
    for b in range(BATCH):
        full = np.concatenate([res[b]["xo0"].T, res[b]["xo1"].T], axis=0)
        out[b] = full[CTX:]
    return out
```

```python
import contextlib
import numpy as np
import ml_dtypes
import concourse.bass as bass
import concourse.mybir as mybir
from concourse.bass_utils import run_bass_kernel_spmd

F32 = mybir.dt.float32
BF16 = mybir.dt.bfloat16
AF = mybir.ActivationFunctionType
ALU = mybir.AluOpType
NPBF = ml_dtypes.bfloat16

D_MODEL = 1024
DEPTH = 4
BATCH = 4
SEQ = 4096
CTX = 256
LSEQ = SEQ + CTX
TOK = LSEQ // 2
D_FF = 2816
D_IN = 3328
NCORES = 8
RMS_EPS = 1e-6
TT = [(0, 512), (512, 512), (1024, 512), (1536, 512), (2048, 128)]
HALVES = [[0, 1], [2, 3, 4]]


def segs(c0, w):
    out = []
    if c0 < CTX:
        out.append((c0, min(c0 + w, CTX), 0))
    if c0 + w > CTX:
        out.append((max(c0, CTX), c0 + w, 1))
    return out


class Res:
    __slots__ = ("name", "w", "r")

    def __init__(self, name=""):
        self.name = name
        self.w = None
        self.r = []


class Ring:
    def __init__(self, tiles):
        self.tiles = tiles
        self.res = [Res() for _ in tiles]
        self.i = 0

    def next(self):
        i = self.i % len(self.tiles)
        self.i += 1
        return self.tiles[i], self.res[i], i


class Sched:
    ENGS = ("pe", "act", "dve", "pool", "sp")

    def __init__(self, nc, same_eng_sync=True):
        self.nc = nc
        self.ops = {e: [] for e in self.ENGS}
        self.dma_cnt = {}
        self.same = same_eng_sync
        self.known = {e: {} for e in self.ENGS}
        self.bar = {}

    def barrier(self):
        last = {}
        for e in self.ENGS:
            for i in range(len(self.ops[e]) - 1, -1, -1):
                if self.ops[e][i]["tok"][0] == "e":
                    last[e] = i
                    break
        for e in self.ENGS:
            need = self.bar.setdefault(e, {})
            for f, idx in last.items():
                if f != e:
                    need[("e", f)] = max(need.get(("e", f), -1), idx)
            for k, c in self.dma_cnt.items():
                need[("d", k)] = max(need.get(("d", k), -1), c)

    def op(self, eng, emit, reads=(), writes=(), dma_key=None):
        lst = self.ops[eng]
        idx = len(lst)
        need = {}

        def add(tok):
            if tok is None:
                return
            kind, src, val = tok
            if kind == "e" and src == eng and (eng == "pe" or not self.same):
                return
            if kind == "d":
                val = self.dma_cnt[src]
            k = (kind, src)
            if need.get(k, -1) < val:
                need[k] = val

        for r in reads:
            add(r.w)
        for w in writes:
            add(w.w)
            for t in w.r:
                add(t)
        for k, v in self.bar.pop(eng, {}).items():
            if need.get(k, -1) < v:
                need[k] = v
        kn = self.known[eng]
        waits = []
        for k, v in need.items():
            if kn.get(k, -1) >= v:
                continue
            kn[k] = v
            waits.append((k, v))
        if dma_key is not None:
            c = self.dma_cnt.get(dma_key, 0) + 1
            self.dma_cnt[dma_key] = c
            tok = ("d", dma_key, c)
        else:
            tok = ("e", eng, idx)
        lst.append({"emit": emit, "waits": waits, "tok": tok, "sig": False})
        for r in reads:
            r.r.append(tok)
        for w in writes:
            w.w = tok
            w.r = []
        return tok

    def mm(self, out, lhsT, rhs, start, stop, reads, writes):
        self.op("pe", lambda h: h.matmul(out, lhsT=lhsT, rhs=rhs, start=start, stop=stop), reads, writes)

    def transpose(self, out, in_, ident, reads, writes):
        self.op("pe", lambda h: h.transpose(out, in_, ident), reads, writes)

    def act(self, out, in_, func, reads, writes, bias=None, scale=None):
        kw = {}
        if bias is not None:
            kw["bias"] = bias
        if scale is not None:
            kw["scale"] = scale
        self.op("act", lambda h: h.activation(out=out, in_=in_, func=func, **kw), reads, writes)

    def tt(self, eng, out, in0, in1, op, reads, writes):
        self.op(eng, lambda h: h.tensor_tensor(out=out, in0=in0, in1=in1, op=op), reads, writes)

    def ts(self, eng, out, in0, s1, s2, op0, op1, reads, writes):
        if op1 is None:
            self.op(eng, lambda h: h.tensor_scalar(out=out, in0=in0, scalar1=s1, scalar2=None, op0=op0), reads, writes)
        else:
            self.op(eng, lambda h: h.tensor_scalar(out=out, in0=in0, scalar1=s1, scalar2=s2, op0=op0, op1=op1), reads, writes)

    def stt(self, eng, out, in0, scalar, in1, op0, op1, reads, writes):
        self.op(eng, lambda h: h.scalar_tensor_tensor(out=out, in0=in0, scalar=scalar, in1=in1, op0=op0, op1=op1), reads, writes)

    def recip(self, out, in_, reads, writes):
        self.op("dve", lambda h: h.reciprocal(out=out, in_=in_), reads, writes)

    def copy(self, eng, out, in_, reads, writes):
        self.op(eng, lambda h: h.tensor_copy(out=out, in_=in_), reads, writes)

    def memset(self, eng, ap, val, writes):
        self.op(eng, lambda h: h.memset(ap, val), (), writes)

    def dma(self, q, out, in_, reads, writes, key):
        self.op(q, lambda h: h.dma_start(out=out, in_=in_), reads, writes, dma_key=key)

    def emit_all(self, final_keys=()):
        nc = self.nc
        for e in self.ENGS:
            for o in self.ops[e]:
                for (kind, src), v in o["waits"]:
                    if kind == "e":
                        self.ops[src][v]["sig"] = True
        rank = {}
        for e in self.ENGS:
            c = 0
            rk = []
            for o in self.ops[e]:
                if o["sig"]:
                    c += 1
                rk.append(c)
            rank[e] = rk
        with contextlib.ExitStack() as st:
            esem = {e: st.enter_context(nc.semaphore("s_" + e)) for e in self.ENGS}
            dsem = {k: st.enter_context(nc.semaphore("d_" + str(k))) for k in self.dma_cnt}
            block = st.enter_context(nc.Block())

            def run(e, h):
                for o in self.ops[e]:
                    for (kind, src), v in o["waits"]:
                        if kind == "e":
                            h.wait_ge(esem[src], rank[src][v])
                        else:
                            h.wait_ge(dsem[src], 16 * v)
                    ins = o["emit"](h)
                    if o["tok"][0] == "d":
                        ins.then_inc(dsem[o["tok"][1]], 16)
                    elif o["sig"]:
                        ins.then_inc(esem[e], 1)
                if e == "sp":
                    for k in (self.dma_cnt if final_keys == "all" else final_keys):
                        if k in self.dma_cnt:
                            h.wait_ge(dsem[k], 16 * self.dma_cnt[k])

            @block.tensor
            def _(h):
                run("pe", h)

            @block.scalar
            def _(h):
                run("act", h)

            @block.vector
            def _(h):
                run("dve", h)

            @block.gpsimd
            def _(h):
                run("pool", h)

            @block.sync
            def _(h):
                run("sp", h)
        self.stats = {e: (len(self.ops[e]), rank[e][-1] if rank[e] else 0) for e in self.ENGS}


NMODCH = DEPTH * 9 * D_MODEL // 128


def emit_mod(nc, S, io, tag):
    with contextlib.ExitStack() as st:
        sb = lambda name, shape, dt=F32: st.enter_context(nc.sbuf_tensor(tag + name, shape, dt))
        c_sb = sb("c_sb", [128, 8, 2])
        sc_sb = sb("sc_sb", [128, 8, 2])
        b_sb = sb("b_sb", [128, NMODCH])
        o_sb = sb("o_sb", [128, 2, NMODCH])
        wring = Ring([sb(f"w{i}", [128, 8, 512]) for i in range(3)])
        psr = Ring([st.enter_context(nc.psum_tensor(tag + f"ps{i}", [128, 8], F32)) for i in range(4)])
        Rc, Rsc, Rb, Ro = Res(), Res(), Res(), Res()
        S.dma("sp", c_sb[:], io["cT"], (), [Rc], "c")
        S.dma("sp", b_sb[:], io["bm"], (), [Rb], "b")
        S.act(sc_sb[:], c_sb[:], AF.Silu, [Rc], [Rsc])
        for g in range(NMODCH // 4):
            l, col = (g * 4) // 72, ((g * 4) % 72) * 128
            w, rw, wi = wring.next()
            S.dma("sp", w[:], io["w_mod"][l].rearrange("(kc p) n -> p kc n", p=128)[:, :, col:col + 512], (), [rw], f"w{wi}")
            for jj in range(4):
                j = g * 4 + jj
                ps, rps, _ = psr.next()
                for k in range(8):
                    S.mm(ps[:, 0:2], w[:, k, jj * 128:(jj + 1) * 128], sc_sb[:, k, :], k == 0, k == 7, [rw, Rsc], [rps])
                S.act(o_sb[:, :, j], ps[:, 0:2], AF.Identity, [rps, Rb], [Ro], bias=b_sb[:, j:j + 1])
        S.dma("sp", io["mod_all"], o_sb[:], [Ro], (), "out")
        S.barrier()


SL_G5, SL_A2, SL_S2, SL_G8, SL_A0, SL_S0, SL_G2, SL_A1, SL_S1 = range(9)
NSL = 9


def emit_tl(nc, S, io, stageA, stageB, half, tag):
    xT, xT_o = io["xT_in"], io["xT_out"]
    if stageA:
        mixT, w_out, up2, dn2, ngP = io["mixT"], io["w_out"], io["up2"], io["dn2"], io["ngP"]
    if stageB:
        up1, dn1, w_in, qkg, ngC = io["up1"], io["dn1"], io["w_in"], io["qkg"], io["ngC"]
        qT_o, kT_o, v_o, rT_o = io["qT_o"], io["kT_o"], io["v_o"], io["rT_o"]
    mrows = (1, 0) if half == 0 else (0, 0)
    kc = lambda ap: ap.rearrange("(kc p) n -> p kc n", p=128)

    with contextlib.ExitStack() as st:
        sb = lambda name, shape, dt=F32: st.enter_context(nc.sbuf_tensor(tag + name, shape, dt))
        x_sb = sb("x_sb", [128, 8, TOK])
        nx_sb = sb("nx_sb", [128, 8, TOK], BF16)
        g_sb = sb("g_sb", [128, 22, 1152], BF16)
        wup_r = Ring([sb(f"wup{i}", [128, 8, 256], BF16) for i in range(2)])
        wup_res2 = [Res() for _ in range(2)]
        wdn_r = Ring([sb(f"wdn{i}", [128, 22, 128], BF16) for i in range(2)])
        sq_r = Ring([sb(f"sq{i}", [128, 512], BF16) for i in range(3)])
        rs_r = Ring([sb(f"rs{i}", [128, 512]) for i in range(2)])
        tmp_r = Ring([sb(f"tmp{i}", [128, 512]) for i in range(3)])
        sg_r = Ring([sb(f"sg{i}", [128, 512]) for i in range(2)])
        sc = sb("sc", [128, 2, NSL, 8])
        modP_sb = sb("modP_sb", [128, 2, 9, 8])
        modC_sb = sb("modC_sb", [128, 2, 9, 8])
        ngP_sb = sb("ngP_sb", [128, 3, 8])
        ngC_sb = sb("ngC_sb", [128, 3, 8])
        qkg_sb = sb("qkg_sb", [128, 2])
        qkg8 = sb("qkg8", [128, 2])
        ones_bf = sb("ones_bf", [128, 128], BF16)
        bd64 = sb("bd64", [128, 128], BF16)
        epsN = sb("epsN", [128, 1])
        epsQ = sb("epsQ", [128, 1])
        if stageB:
            wv_sb = sb("wv_sb", [128, 8, 512], BF16)
            stf_r = Ring([sb(f"stf{i}", [128, 512]) for i in range(2)])
            stb_r = Ring([sb(f"stb{i}", [128, 512], BF16) for i in range(2)])
        pst = [st.enter_context(nc.psum_tensor(tag + f"ps{i}", [128, 512], F32)) for i in range(8)]
        psg = Ring(pst[0:2])
        psu = Ring(pst[2:4])
        pso = Ring(pst[4:6])
        psn = Ring(pst[6:8])

        Rx = [[Res() for _ in TT] for _ in range(8)]
        Rnx = [[Res() for _ in TT] for _ in range(8)]
        Rg = [[Res() for _ in TT] for _ in range(22)]
        Rsc, Rconst, Rmod = Res(), Res(), Res()

        S.memset("dve", ones_bf[:], 1.0, [Rconst])
        S.memset("dve", bd64[:], 0.0, [Rconst])
        S.memset("dve", bd64[0:64, 0:64], 1.0, [Rconst])
        S.memset("dve", bd64[64:128, 64:128], 1.0, [Rconst])
        S.memset("dve", epsN[:], D_MODEL * RMS_EPS, [Rconst])
        S.memset("dve", epsQ[:], 64 * RMS_EPS, [Rconst])
        for k in range(8):
            for n, (c0, w) in enumerate(TT):
                S.dma("sp", x_sb[:, k, c0:c0 + w], xT[k * 128:(k + 1) * 128, c0:c0 + w], (), [Rx[k][n]], "xin")
        def ld_mod(dst, l):
            for ty in range(2):
                S.dma("sp", dst[:, ty], io["mod_all"][:, mrows[ty], l * 72:(l + 1) * 72].rearrange("p (m c) -> p m c", c=8),
                      (), [Rmod], "mods")
        if stageA:
            ld_mod(modP_sb, io["lA"])
            S.dma("sp", ngP_sb[:], ngP, (), [Rmod], "mods")
        if stageB:
            ld_mod(modC_sb, io["lB"])
            S.dma("sp", ngC_sb[:], ngC, (), [Rmod], "mods")
            S.dma("sp", qkg_sb[:], qkg, (), [Rmod], "mods")

        def mk_A(slot, mod_sb, ng_sb, gi, mi_scale):
            for ty in range(2):
                S.ts("dve", sc[:, ty, slot, :], mod_sb[:, ty, mi_scale, :], 1.0, 32.0, ALU.add, ALU.mult, [Rmod], [Rsc])
                S.tt("dve", sc[:, ty, slot, :], sc[:, ty, slot, :], ng_sb[:, gi, :], ALU.mult, [Rmod, Rsc], [Rsc])

        def mk_cp(slot, mod_sb, mi, mul):
            for ty in range(2):
                S.ts("dve", sc[:, ty, slot, :], mod_sb[:, ty, mi, :], mul, None, ALU.mult, None, [Rmod], [Rsc])

        if stageA:
            mk_cp(SL_G5, modP_sb, 5, 1.0)
            mk_A(SL_A2, modP_sb, ngP_sb, 2, 7)
            mk_cp(SL_S2, modP_sb, 6, 1.0)
            mk_cp(SL_G8, modP_sb, 8, 0.5)
        if stageB:
            mk_A(SL_A0, modC_sb, ngC_sb, 0, 1)
            mk_cp(SL_S0, modC_sb, 0, 1.0)
            mk_cp(SL_G2, modC_sb, 2, 0.5)
            mk_A(SL_A1, modC_sb, ngC_sb, 1, 4)
            mk_cp(SL_S1, modC_sb, 3, 1.0)
            S.ts("dve", qkg8[:, 0:1], qkg_sb[:, 0:1], 1.0, None, ALU.mult, None, [Rmod], [Rsc])
            S.ts("dve", qkg8[:, 1:2], qkg_sb[:, 1:2], 8.0, None, ALU.mult, None, [Rmod], [Rsc])

        def norm_mod(slA, slS):
            for n, (c0, w) in enumerate(TT):
                ps, rps, _ = psn.next()
                for k in range(8):
                    sq, rsq, _ = sq_r.next()
                    S.act(sq[:, :w], x_sb[:, k, c0:c0 + w], AF.Square, [Rx[k][n]], [rsq])
                    S.mm(ps[:, :w], ones_bf[:], sq[:, :w], k == 0, k == 7, [rsq, Rconst], [rps])
                rs, rrs, _ = rs_r.next()
                S.act(rs[:, :w], ps[:, :w], AF.Sqrt, [rps, Rconst], [rrs], bias=epsN[:, 0:1])
                S.recip(rs[:, :w], rs[:, :w], [rrs], [rrs])
                for k in range(8):
                    for (a, b, ty) in segs(c0, w):
                        tmp, rtmp, _ = tmp_r.next()
                        S.stt("dve", tmp[:, :b - a], x_sb[:, k, a:b], sc[:, ty, slA, k:k + 1], rs[:, a - c0:b - c0],
                              ALU.mult, ALU.mult, [Rx[k][n], rrs, Rsc], [rtmp])
                        S.ts("pool", nx_sb[:, k, a:b], tmp[:, :b - a], sc[:, ty, slS, k:k + 1], None, ALU.add, None,
                             [rtmp, Rsc], [Rnx[k][n]])

        def ffn(up_ap, dn_ap, slG):
            up_r = kc(up_ap)
            dn_r = kc(dn_ap)
            for half in HALVES:
                h0 = TT[half[0]][0]
                for j in range(22):
                    wu, rwu, wi = wup_r.next()
                    rwu2 = wup_res2[wi]
                    S.dma("pool", wu[:, :, 0:128], up_r[:, :, j * 128:(j + 1) * 128], (), [rwu], f"wup{wi}")
                    S.dma("pool", wu[:, :, 128:256], up_r[:, :, D_FF + j * 128:D_FF + (j + 1) * 128], (), [rwu2], f"wup{wi}")
                    for n in half:
                        c0, w = TT[n]
                        pg, rpg, _ = psg.next()
                        pu, rpu, _ = psu.next()
                        for k in range(8):
                            S.mm(pg[:, :w], wu[:, k, 0:128], nx_sb[:, k, c0:c0 + w], k == 0, k == 7, [rwu, Rnx[k][n]], [rpg])
                        for k in range(8):
                            S.mm(pu[:, :w], wu[:, k, 128:256], nx_sb[:, k, c0:c0 + w], k == 0, k == 7, [rwu2, Rnx[k][n]], [rpu])
                        sg, rsg, _ = sg_r.next()
                        S.act(sg[:, :w], pg[:, :w], AF.Silu, [rpg], [rsg])
                        S.tt("dve", g_sb[:, j, c0 - h0:c0 - h0 + w], sg[:, :w], pu[:, :w], ALU.mult, [rsg, rpu], [Rg[j][n]])
                for i in range(8):
                    wd, rwd, wi = wdn_r.next()
                    S.dma("pool", wd[:], dn_r[:, :, i * 128:(i + 1) * 128], (), [rwd], f"wdn{wi}")
                    for n in half:
                        c0, w = TT[n]
                        po, rpo, _ = pso.next()
                        for j in range(22):
                            S.mm(po[:, :w], wd[:, j, :], g_sb[:, j, c0 - h0:c0 - h0 + w], j == 0, j == 21, [rwd, Rg[j][n]], [rpo])
                        for (a, b, ty) in segs(c0, w):
                            S.stt("dve", x_sb[:, i, a:b], po[:, a - c0:b - c0], sc[:, ty, slG, i:i + 1], x_sb[:, i, a:b],
                                  ALU.mult, ALU.add, [rpo, Rx[i][n], Rsc], [Rx[i][n]])

        if stageA:
            for k in range(8):
                for n, (c0, w) in enumerate(TT):
                    S.dma("sp", nx_sb[:, k, c0:c0 + w], mixT[k * 128:(k + 1) * 128, c0:c0 + w], (), [Rnx[k][n]], "mixin")
            wo_r = kc(w_out)
            for i in range(8):
                wu, rwu, wi = wup_r.next()
                S.dma("pool", wu[:, :, 0:128], wo_r[:, :, i * 128:(i + 1) * 128], (), [rwu], f"wup{wi}")
                for n, (c0, w) in enumerate(TT):
                    po, rpo, _ = pso.next()
                    for k in range(8):
                        S.mm(po[:, :w], wu[:, k, 0:128], nx_sb[:, k, c0:c0 + w], k == 0, k == 7, [rwu, Rnx[k][n]], [rpo])
                    for (a, b, ty) in segs(c0, w):
                        S.stt("dve", x_sb[:, i, a:b], po[:, a - c0:b - c0], sc[:, ty, SL_G5, i:i + 1], x_sb[:, i, a:b],
                              ALU.mult, ALU.add, [rpo, Rx[i][n], Rsc], [Rx[i][n]])
            norm_mod(SL_A2, SL_S2)
            ffn(up2, dn2, SL_G8)

        if stageB:
            norm_mod(SL_A0, SL_S0)
            ffn(up1, dn1, SL_G2)

        for k in range(8):
            S.dma("sp", xT_o[k * 128:(k + 1) * 128, :], x_sb[:, k, :], [Rx[k][n] for n in range(len(TT))], (), "out")

        if stageB:
            norm_mod(SL_A1, SL_S1)
            wi_r = kc(w_in)
            Rwv = Res()
            S.dma("pool", wv_sb[:], wi_r[:, :, 1024:1536], (), [Rwv], "wv")
            chunks = [("q", c, c * 128) for c in range(4)] + [("k", c, 512 + c * 128) for c in range(4)] + \
                     [("r", c, 1536 + c * 128) for c in range(14)]
            for (kind, c, col) in chunks:
                wu, rwu, wi = wup_r.next()
                S.dma("pool", wu[:, :, 0:128], wi_r[:, :, col:col + 128], (), [rwu], f"wup{wi}")
                for n, (c0, w) in enumerate(TT):
                    po, rpo, _ = pso.next()
                    for k in range(8):
                        S.mm(po[:, :w], wu[:, k, 0:128], nx_sb[:, k, c0:c0 + w], k == 0, k == 7, [rwu, Rnx[k][n]], [rpo])
                    if kind == "r":
                        stf, rstf, si = stf_r.next()
                        S.act(stf[:, :w], po[:, :w], AF.Identity, [rpo], [rstf])
                        S.dma("sp", rT_o[c * 128:(c + 1) * 128, c0:c0 + w], stf[:, :w], [rstf], (), f"stf{si}")
                    else:
                        sq, rsq, _ = sq_r.next()
                        S.act(sq[:, :w], po[:, :w], AF.Square, [rpo], [rsq])
                        ps, rps, _ = psn.next()
                        S.mm(ps[:, :w], bd64[:], sq[:, :w], True, True, [rsq, Rconst], [rps])
                        rs, rrs, _ = rs_r.next()
                        S.act(rs[:, :w], ps[:, :w], AF.Sqrt, [rps, Rconst], [rrs], bias=epsQ[:, 0:1])
                        S.recip(rs[:, :w], rs[:, :w], [rrs], [rrs])
                        stb, rstb, si = stb_r.next()
                        gi = 0 if kind == "q" else 1
                        S.stt("dve", stb[:, :w], po[:, :w], qkg8[:, gi:gi + 1], rs[:, :w], ALU.mult, ALU.mult,
                              [rpo, rrs, Rsc], [rstb])
                        dst = qT_o if kind == "q" else kT_o
                        S.dma("sp", dst[c * 128:(c + 1) * 128, c0:c0 + w], stb[:, :w], [rstb], (), f"stb{si}")
            for t in range(TOK // 128):
                n = min(t // 4, 4)
                po, rpo, _ = pso.next()
                for k in range(8):
                    S.mm(po[:, :], nx_sb[:, k, t * 128:(t + 1) * 128], wv_sb[:, k, :], k == 0, k == 7, [Rwv, Rnx[k][n]], [rpo])
                stb, rstb, si = stb_r.next()
                S.act(stb[:, :], po[:, :], AF.Identity, [rpo], [rstb])
                S.dma("sp", v_o[t * 128:(t + 1) * 128, :], stb[:, :], [rstb], (), f"stb{si}")
        S.barrier()


NCH = LSEQ // 64
NT128 = LSEQ // 128
BLOCKS = [(0, 256)] + [(256 + 512 * i, 512) for i in range(8)]
NEG = -30000.0


def emit_mx(nc, S, io, tag, do_attn=True, do_rwkv=True, debug=False):
    identD = io["ident"]
    if do_attn:
        qT, kT, vtok, nab, namask, attn_o = io["qT"], io["kT"], io["vtok"], io["nab"], io["namask"], io["attn_o"]
    with contextlib.ExitStack() as st0:
        sb0 = lambda name, shape, dt=F32: st0.enter_context(nc.sbuf_tensor(tag + name, shape, dt))
        ident_f = sb0("ident_f", [128, 128])
        ident_b = sb0("ident_b", [128, 128], BF16)
        Rid = Res()
        S.dma("sp", ident_f[:], identD, (), [Rid], "c0")
        S.copy("dve", ident_b[:], ident_f[:], [Rid], [Rid])
        final_keys = []

        if do_attn:
            with contextlib.ExitStack() as st:
                sb = lambda name, shape, dt=F32: st.enter_context(nc.sbuf_tensor(tag + name, shape, dt))
                q_sb = sb("q_sb", [64, 4, LSEQ], BF16)
                k_sb = sb("k_sb", [64, 4, LSEQ], BF16)
                v_e = sb("v_e", [128, NT128, 4, 80], BF16)
                v_od = sb("v_od", [128, NT128 - 1, 4, 80], BF16)
                tu = sb("tu", [128, 4, 14, 64])
                msk = sb("msk", [128, 64])
                at_sb = sb("at_sb", [128, 2, LSEQ], BF16)
                sbias_r = Ring([sb(f"sbias{i}", [128, 256]) for i in range(2)])
                p_r = Ring([sb(f"p{i}", [128, 4, 64], BF16) for i in range(12)])
                rc_r = Ring([sb(f"rc{i}", [64, 4, 1]) for i in range(2)])
                on_r = Ring([sb(f"on{i}", [64, 256], BF16) for i in range(2)])
                pss_r = Ring([st.enter_context(nc.psum_tensor(tag + f"pss{i}", [128, 512], F32))[:, 0:256] for i in range(3)])
                pso_r = Ring([st.enter_context(nc.psum_tensor(tag + f"pso{i}", [128, 512], F32))[0:64, 0:264].rearrange("p (h e) -> p h e", e=66) for i in range(2)])
                pst_r = Ring([st.enter_context(nc.psum_tensor(tag + f"pst{i}", [128, 1024], BF16))[:, 0:64] for i in range(2)])
                Rq, Rk, Rve, Rvo, Rtu = Res(), Res(), Res(), Res(), Res()
                Rat_rows = []
                for hl in range(4):
                    S.dma("sp", q_sb[:, hl, :], qT[hl * 64:(hl + 1) * 64, :], (), [Rq], "aq")
                    S.dma("sp", k_sb[:, hl, :], kT[hl * 64:(hl + 1) * 64, :], (), [Rk], "ak")
                S.memset("dve", v_e[:, :, :, 64:80], 0.0, [Rve])
                S.memset("dve", v_e[:, :, :, 64:65], 1.0, [Rve])
                S.memset("dve", v_od[:, :, :, 64:80], 0.0, [Rvo])
                S.memset("dve", v_od[:, :, :, 64:65], 1.0, [Rvo])
                ve_src = vtok.rearrange("(n p) (h d) -> p n h d", p=128, h=4)
                vo_src = vtok[64:64 + 128 * (NT128 - 1), :].rearrange("(n p) (h d) -> p n h d", p=128, h=4)
                for hl in range(4):
                    for n0 in range(0, NT128, 17):
                        n1 = min(n0 + 17, NT128)
                        S.dma("sp", v_e[:, n0:n1, hl, 0:64], ve_src[:, n0:n1, hl, :], (), [Rve], "ave")
                        n1o = min(n0 + 17, NT128 - 1)
                        S.dma("sp", v_od[:, n0:n1o, hl, 0:64], vo_src[:, n0:n1o, hl, :], (), [Rvo], "avo")
                S.dma("sp", tu[:], nab, (), [Rtu], "atu")
                S.dma("sp", msk[:], namask, (), [Rtu], "atu")
                for hl in range(4):
                    S.tt("dve", tu[:, hl], tu[:, hl], msk[:].unsqueeze(1).to_broadcast([128, 14, 64]), ALU.add, [Rtu], [Rtu])

                import os
                lvl = int(os.environ.get("MXLVL", "9"))

                def qrow(t0, ktiles):
                    if lvl < 1:
                        return
                    ptl = []
                    for (koff, vt_ap, d0) in ktiles:
                        ps, rps, _ = pss_r.next()
                        for hl in range(4):
                            S.mm(ps[:, hl * 64:(hl + 1) * 64], k_sb[:, hl, koff:koff + 128],
                                 q_sb[:, hl, t0:t0 + 64], True, True, [Rq, Rk], [rps])
                        p, rp, _ = p_r.next()
                        if d0 is None:
                            S.act(p[:], ps.rearrange("p (h q) -> p h q", h=4), AF.Exp, [rps], [rp])
                        else:
                            sbt, rsb, _ = sbias_r.next()
                            S.tt("dve", sbt[:].rearrange("p (h q) -> p h q", h=4), ps.rearrange("p (h q) -> p h q", h=4),
                                 tu[:, :, d0, :], ALU.add, [rps, Rtu], [rsb])
                            S.act(p[:], sbt[:].rearrange("p (h q) -> p h q", h=4), AF.Exp, [rsb], [rp])
                        ptl.append((p, rp, vt_ap))
                    if lvl < 2:
                        return
                    po, rpo, _ = pso_r.next()
                    for hl in range(4):
                        for i, (p, rp, vt_ap) in enumerate(ptl):
                            S.mm(po[:, hl, :], p[:, hl, :], vt_ap[:, hl, 0:66], i == 0, i == len(ptl) - 1, [rp, Rve, Rvo], [rpo])
                    if lvl < 3:
                        return
                    rc, rrc, _ = rc_r.next()
                    S.recip(rc[:], po[:, :, 64:65], [rpo], [rrc])
                    on, ron, _ = on_r.next()
                    S.tt("dve", on[:].rearrange("p (h d) -> p h d", h=4), po[:, :, 0:64], rc[:].to_broadcast([64, 4, 64]),
                         ALU.mult, [rpo, rrc], [ron])
                    if lvl < 4:
                        return
                    for c2 in range(2):
                        pt, rpt, _ = pst_r.next()
                        S.transpose(pt[:, :], on[:, c2 * 128:(c2 + 1) * 128], ident_b[0:64, 0:64], [ron, Rid], [rpt])
                        rr = Res()
                        Rat_rows.append(rr)
                        S.act(at_sb[:, c2, t0:t0 + 64], pt[:, :], AF.Identity, [rpt], [rr])

                ctx_tiles = [(0, v_e[:, 0], None), (128, v_e[:, 1], None)]
                for qi in range(4):
                    qrow(qi * 64, ctx_tiles)
                for i in range(64 if lvl >= 9 else int(os.environ.get('MXROWS', '1'))):
                    start = min(max(i - 4, 0), 56)
                    tiles = []
                    for m in range(4):
                        r = start + 2 * m
                        vt_ap = v_e[:, 2 + r // 2] if r % 2 == 0 else v_od[:, (3 + r) // 2]
                        tiles.append((CTX + 64 * r, vt_ap, r - i + 7))
                    qrow(CTX + 64 * i, tiles + ctx_tiles)
                for c2 in range(2):
                    S.dma("sp", attn_o[c2 * 128:(c2 + 1) * 128, :], at_sb[:, c2, :], Rat_rows, (), "aout")
                S.barrier()

        if do_rwkv:
            L = dict(io)
            L.update({"ident_b": ident_b, "ident_f": ident_f, "Rid": Rid, "tag": tag})
            _mx_rwkv(nc, S, st0, L, final_keys, debug)
        S.barrier()


def _na_tables(na_bias_l, g):
    wk = np.arange(64)[:, None]
    wq = np.arange(64)[None, :]
    dc = np.clip(wk - wq, -15, 15) + 15
    cs = np.clip(np.arange(64) - 8, 0, 48)[None, :]
    inwin = (wk >= cs) & (wk < cs + 16)
    tab = np.empty((2, 64, 4, 14, 64), np.float32)
    for rl in range(2):
        for hl in range(4):
            for d0 in range(14):
                tab[rl, :, hl, d0, :] = na_bias_l[g * 4 + hl, d0 + rl][dc]
    mask = np.where(inwin, 0.0, NEG).astype(np.float32)
    return np.ascontiguousarray(tab.reshape(128, 4, 14, 64)), np.ascontiguousarray(np.concatenate([mask, mask], 0))


LNX_EPS = 64e-5
TRI_SEL = [(0, 1, 3), (2, 3, 1)]
AX = mybir.AxisListType


def _mx_rwkv(nc, S, st0, L, final_keys, debug):
    rrow, mu, pp, w0row, w2, a2, g2, lnrow, tri, amask, lmask, hsel, rw_o = (
        L[k] for k in ("rrow", "mu", "pp", "w0row", "w2", "a2", "g2", "lnrow", "tri", "amask", "lmask", "hsel", "rw_o"))
    tag = L["tag"]
    ident_b, ident_f, Rid = L["ident_b"], L["ident_f"], L["Rid"]
    with contextlib.ExitStack() as st:
        sb = lambda name, shape, dt=F32: st.enter_context(nc.sbuf_tensor(tag + name, shape, dt))
        ps_t = [st.enter_context(nc.psum_tensor(tag + f"rps{i}", [128, 512], F32)) for i in range(8)]
        rs = sb("rs", [128, 2, LSEQ], BF16)
        ks = sb("ks", [128, 2, LSEQ], BF16)
        kk = sb("kk", [128, 2, LSEQ], BF16)
        vt = sb("vt", [128, NT128, 256], BF16)
        lora6 = sb("lora6", [128, LSEQ], BF16)
        sdg = sb("sdg", [128, LSEQ], BF16)
        mu_sb = sb("mu_sb", [128, 8]); om_sb = sb("om_sb", [128, 8]); hm_sb = sb("hm_sb", [128, 8])
        pp_sb = sb("pp_sb", [128, 2, 8]); omka = sb("omka", [128, 2]); omka2 = sb("omka2", [128, 2])
        w0_f = sb("w0_f", [1, 2, 256]); w0_b = sb("w0_b", [1, 2, 256], BF16)
        ones_row = sb("ones_row", [1, 128], BF16)
        w2z = sb("w2z", [128, 2, 256], BF16)
        a2z = sb("a2z", [128, 2, 256], BF16)
        g2_sb = sb("g2_sb", [128, 256], BF16)
        lng = sb("lng", [128, 256]); lnb = sb("lnb", [128, 256])
        tri_sb = sb("tri_sb", [128, 4, 128])
        amask_sb = sb("amask_sb", [128, 2, 256]); lmask_sb = sb("lmask_sb", [128, 2, 64])
        hsel_f = sb("hsel_f", [128, 2]); hsel_b = sb("hsel_b", [128, 2], BF16)
        eps24 = sb("eps24", [128, 1]); epsln = sb("epsln", [128, 1])
        Rp = Res()
        KEY_K, KEY_A, BON_U, A0F, A0B = 0, 1, 2, 3, 4
        S.dma("sp", mu_sb[:], mu, (), [Rp], "rp")
        S.dma("sp", pp_sb[:], pp, (), [Rp], "rp")
        S.dma("sp", w0_f[:], w0row, (), [Rp], "rp")
        S.dma("sp", lng[:], lnrow[:, 0, :].partition_broadcast(128), (), [Rp], "rp")
        S.dma("sp", lnb[:], lnrow[:, 1, :].partition_broadcast(128), (), [Rp], "rp")
        S.dma("sp", tri_sb[:], tri, (), [Rp], "rp")
        S.dma("sp", amask_sb[:], amask, (), [Rp], "rp")
        S.dma("sp", lmask_sb[:], lmask, (), [Rp], "rp")
        S.dma("sp", hsel_f[:], hsel, (), [Rp], "rp")
        S.memset("dve", w2z[:], 0.0, [Rp])
        S.memset("dve", a2z[:], 0.0, [Rp])
        S.dma("pool", w2z[0:64], w2, [Rp], [Rp], "rpc")
        S.dma("pool", a2z[64:128], a2, [Rp], [Rp], "rpc")
        S.dma("pool", g2_sb[:], g2, (), [Rp], "rpc")
        S.memset("dve", ones_row[:], 1.0, [Rp])
        S.memset("dve", eps24[:], 1e-24, [Rp])
        S.memset("dve", epsln[:], LNX_EPS, [Rp])
        S.copy("dve", w0_b[:], w0_f[:], [Rp], [Rp])
        S.copy("dve", hsel_b[:], hsel_f[:], [Rp], [Rp])
        S.ts("dve", om_sb[:], mu_sb[:], -1.0, 1.0, ALU.mult, ALU.add, [Rp], [Rp])
        S.ts("dve", hm_sb[:], mu_sb[:], 0.5, None, ALU.mult, None, [Rp], [Rp])
        S.ts("dve", omka[:], pp_sb[:, :, KEY_A], -1.0, 1.0, ALU.mult, ALU.add, [Rp], [Rp])
        S.ts("dve", omka2[:], omka[:], 2.0, None, ALU.mult, None, [Rp], [Rp])
        na0 = sb("na0", [128, 2, 2])
        S.ts("dve", na0[:], pp_sb[:, :, A0F:A0F + 2], -1.0, None, ALU.mult, None, [Rp], [Rp])

        def sig_finish(out_ap, e_ap, rd, wr):
            S.ts("pool", e_ap, e_ap, 1.0, None, ALU.add, None, rd, rd)
            S.recip(e_ap, e_ap, rd, rd)
            if out_ap is not e_ap:
                S.copy("pool", out_ap, e_ap, rd, wr)
        bd64 = sb("bd64r", [128, 128], BF16)
        S.memset("dve", bd64[:], 0.0, [Rp])
        S.memset("dve", bd64[0:64, 0:64], 1.0, [Rp])
        S.memset("dve", bd64[64:128, 64:128], 1.0, [Rp])
        SEG = [(0, CTX, 0), (CTX + 1, LSEQ + 1, CTX)]

        import os
        rlvl = int(os.environ.get("RWLVL", "9"))
        with contextlib.ExitStack() as s1:
            sb1 = lambda name, shape, dt=F32: s1.enter_context(nc.sbuf_tensor(tag + name, shape, dt))
            raw_r = Ring([sb1(f"raw{i}", [128, LSEQ + 3]) for i in range(2)])
            tmp_r = Ring([sb1(f"stmp{i}", [128, LSEQ + 1]) for i in range(1)])
            t1_r = Ring([sb1(f"st1{i}", [128, LSEQ + 1]) for i in range(2)])
            vsf_r = Ring([sb1(f"vsf{i}", [128, LSEQ], BF16) for i in range(1)])
            ptr = Ring([ps_t[i][:, 0:64].bitcast(BF16) if False else ps_t[i] for i in range(6, 8)])
            for i in range(2):
                for c in (0, CTX + 1, LSEQ + 2):
                    S.memset("dve", raw_r.tiles[i][:, c:c + 1], 0.0, [raw_r.res[i]])
            Rrs, Rks, Rkk, Rvt, Rl6, Rsdg = Res(), Res(), Res(), Res(), Res(), Res()
            for fc in range(8):
                raw, rraw, ri = raw_r.next()
                S.dma("sp", raw[:, 1:CTX + 1], rrow(fc)[:, 0:CTX], (), [rraw], f"raw{ri}")
                S.dma("sp", raw[:, CTX + 2:LSEQ + 2], rrow(fc)[:, CTX:LSEQ], (), [rraw], f"raw{ri}")
                tmp, rtmp, _ = tmp_r.next()
                t1, rt1, _ = t1_r.next()
                S.tt("pool", tmp[:, :], raw[:, 0:LSEQ + 1], raw[:, 2:LSEQ + 3], ALU.add, [rraw], [rtmp])
                S.act(t1[:, :], raw[:, 1:LSEQ + 2], AF.Identity, [rraw, Rp], [rt1], scale=om_sb[:, fc:fc + 1])
                if fc < 4:
                    dst, rd = (rs, Rrs) if fc < 2 else (ks, Rks)
                    for (a, b, tk) in SEG:
                        S.stt("dve", dst[:, fc % 2, tk:tk + b - a], tmp[:, a:b], hm_sb[:, fc:fc + 1], t1[:, a:b],
                              ALU.mult, ALU.add, [rtmp, rt1, Rp], [rd])
                elif fc < 6:
                    vsf, rvsf, _ = vsf_r.next()
                    for (a, b, tk) in SEG:
                        S.stt("dve", vsf[:, tk:tk + b - a], tmp[:, a:b], hm_sb[:, fc:fc + 1], t1[:, a:b],
                              ALU.mult, ALU.add, [rtmp, rt1, Rp], [rvsf])
                    for n in range(NT128):
                        pt, rpt, _ = ptr.next()
                        ptb = pt[:, 0:64].bitcast(BF16)
                        S.transpose(ptb, vsf[:, n * 128:(n + 1) * 128], ident_b[:], [rvsf, Rid], [rpt])
                        if n % 2 == 0:
                            S.act(vt[:, n, (fc - 4) * 128:(fc - 3) * 128], ptb, AF.Identity, [rpt], [Rvt])
                        else:
                            S.copy("dve", vt[:, n, (fc - 4) * 128:(fc - 3) * 128], ptb, [rpt], [Rvt])
                else:
                    for (a, b, tk) in SEG:
                        S.stt("dve", t1[:, a:b], tmp[:, a:b], hm_sb[:, fc:fc + 1], t1[:, a:b],
                              ALU.mult, ALU.add, [rtmp, rt1, Rp], [rt1])
                    for (a, b, tk) in SEG:
                        if fc == 6:
                            S.copy("pool", lora6[64:128, tk:tk + b - a], t1[64:128, a:b], [rt1], [Rl6])
                            S.act(t1[0:64, a:b], t1[0:64, a:b], AF.Exp, [rt1], [rt1], scale=-2.0)
                            S.ts("pool", t1[0:64, a:b], t1[0:64, a:b], 1.0, None, ALU.add, None, [rt1], [rt1])
                            S.recip(t1[0:64, a:b], t1[0:64, a:b], [rt1], [rt1])
                            S.ts("dve", lora6[0:64, tk:tk + b - a], t1[0:64, a:b], 2.0, -1.0, ALU.mult, ALU.add, [rt1], [Rl6])
                        else:
                            S.act(t1[:, a:b], t1[:, a:b], AF.Exp, [rt1], [rt1], scale=-1.0)
                            sig_finish(sdg[:, tk:tk + b - a], t1[:, a:b], [rt1], [Rsdg])
            sq_r = Ring([sb1(f"ksq{i}", [128, 512], BF16) for i in range(2)])
            rt_r = Ring([sb1(f"krt{i}", [128, 512]) for i in range(2)])
            psk = Ring(ps_t[4:6])
            for fc2 in range(2 if rlvl >= 1 else 0):
                for (t0, W) in BLOCKS:
                    sq, rsq, _ = sq_r.next()
                    S.act(sq[:, :W], ks[:, fc2, t0:t0 + W], AF.Square, [Rks, Rp], [rsq], scale=pp_sb[:, fc2, KEY_K:KEY_K + 1])
                    ps, rps, _ = psk.next()
                    S.mm(ps[:, :W], bd64[:], sq[:, :W], True, True, [rsq, Rp], [rps])
                    rt, rrt, _ = rt_r.next()
                    S.act(rt[:, :W], ps[:, :W], AF.Sqrt, [rps, Rp], [rrt], bias=eps24[:, 0:1])
                    S.recip(rt[:, :W], rt[:, :W], [rrt], [rrt])
                    S.stt("dve", kk[:, fc2, t0:t0 + W], ks[:, fc2, t0:t0 + W], pp_sb[:, fc2, KEY_K:KEY_K + 1], rt[:, :W],
                          ALU.mult, ALU.mult, [Rks, rrt, Rp], [Rkk])

        S.barrier()
        with contextlib.ExitStack() as s2:
            sb2 = lambda name, shape, dt=F32: s2.enter_context(nc.sbuf_tensor(tag + name, shape, dt))
            o_acc = sb2("o_acc", [128, NT128, 256])
            rwst_r = Ring([sb2(f"rwst{i}", [128, 128], BF16) for i in range(4)])
            a_sb = sb2("a_sb", [128, 2, 512]); kd = sb2("kd", [128, 2, 512]); bb = sb2("bb", [128, 2, 512])
            tmpb = sb2("tmpb", [128, 512])
            sig_tok = sb2("sig_tok", [128, 4, 256])
            E1 = sb2("E1", [128, 2, 512]); E2 = sb2("E2", [128, 512]); E3 = sb2("E3", [128, 512]); E4 = sb2("E4", [128, 512])
            QRz = [sb2(f"QRz{i}", [128, 2, 8, 128], BF16) for i in range(2)]
            kt = sb2("kt", [128, 2, 512], BF16); bt = sb2("bt", [128, 2, 512], BF16)
            khat = sb2("khat", [128, 512], BF16); bhat = sb2("bhat", [128, 512], BF16)
            khat_tok = [sb2(f"khat_tok{i}", [128, 4, 256], BF16) for i in range(2)]
            bhat_tok = [sb2(f"bhat_tok{i}", [128, 4, 256], BF16) for i in range(2)]
            AT = [sb2(f"AT{i}", [128, 4, 256], BF16) for i in range(2)]
            Lt = [sb2(f"Lt{i}", [128, 4, 64], BF16) for i in range(2)]
            RLs = [[sb2(f"RLs{i}_{lv}", [128, 2, 4, 64], BF16) for lv in range(2)] for i in range(2)]
            Nt = [[sb2(f"Nt{i}_{j}", [128, 4, 64], BF16) for j in range(2)] for i in range(2)]
            Xs = [sb2(f"Xs{i}", [128, 4, 64], BF16) for i in range(2)]
            Us = [sb2(f"Us{i}", [128, 4, 64], BF16) for i in range(2)]
            H = sb2("H", [128, 2, 64]); Hb = sb2("Hb", [128, 2, 64], BF16)
            Rz = Res()
            for t in QRz + khat_tok + bhat_tok + AT + Lt + Xs + Us + [x for y in RLs for x in y] + [x for y in Nt for x in y]:
                S.memset("dve", t[:], 0.0, [Rz])
            RQR = [Res(), Res()]; Rkt, Rbt = Res(), Res()
            Rkht = [Res(), Res()]; Rbht = [Res(), Res()]
            RAT = [Res(), Res()]; RLt = [Res(), Res()]; RRL = [[Res() for _ in range(2)] for _ in range(2)]
            RN = [[Res(), Res()] for _ in range(2)]; RX = [Res(), Res()]; RU = [Res(), Res()]
            RH, RHb, RE1, Ra, Rkd, Rbb, Rsig, Roacc = Res(), Res(), Res(), Res(), Res(), Res(), Res(), Res()
            RE2, RE3, RE4, Rkh, Rbh = Res(), Res(), Res(), Res(), Res()
            Ro = [Res() for _ in range(NT128)]
            pA0, pA1, pLN, pSQ, pXU, pOH, pB0, pB1 = ps_t
            RpA0, RpA1, RpLN, RpSQ, RpXU, RpOH, RpB0, RpB1 = [Res() for _ in range(8)]

            def v4(ap, rows, c0, n):
                return ap[rows[0]:rows[1], c0:c0 + 4 * n].rearrange("p (h e) -> p h e", h=4)

            for d in range(min(2, int(os.environ.get('RWND', '2'))) if rlvl >= 2 else 0):
                i1, i2, i3 = TRI_SEL[d]
                S.memset("dve", H[:], 0.0, [RH])
                S.memset("dve", Hb[:], 0.0, [RHb])
                border = list(range(9)) if d == 0 else [0] + list(range(8, 0, -1))
                border = border[:int(os.environ.get("RWNB", "9"))]
                for bi in border:
                    t0, W = BLOCKS[bi]
                    nck, ntl = W // 64, W // 128
                    for fc2 in range(2):
                        S.mm(pB0[:, :W], a2z[:, d, fc2 * 128:(fc2 + 1) * 128], lora6[:, t0:t0 + W], True, True, [Rp, Rl6], [RpB0])
                        S.act(a_sb[:, fc2, :W], pB0[:, :W], AF.Exp, [RpB0, Rp], [Ra], bias=na0[:, fc2, d:d + 1], scale=-1.0)
                        _e = a_sb[:, fc2, :W]
                        sig_finish(_e, _e, [Ra], [Ra])
                        S.ts("dve", tmpb[:, :W], a_sb[:, fc2, :W], pp_sb[:, fc2, KEY_A:KEY_A + 1], omka[:, fc2:fc2 + 1],
                             ALU.mult, ALU.add, [Ra, Rp], [Rkd])
                        S.tt("dve", kd[:, fc2, :W], ks[:, fc2, t0:t0 + W], tmpb[:, :W], ALU.mult, [Rks, Rkd], [Rkd])
                        S.tt("pool", bb[:, fc2, :W], kk[:, fc2, t0:t0 + W], a_sb[:, fc2, :W], ALU.mult, [Rkk, Ra], [Rbb])
                    for j in range(ntl):
                        S.mm(pB1[:, 0:256], lora6[:, t0 + j * 128:t0 + (j + 1) * 128], w2z[:, d, :], True, False, [Rp, Rl6], [RpB1])
                        S.mm(pB1[:, 0:256], ones_row[0:1, :], w0_b[0:1, d, :], False, True, [Rp], [RpB1])
                        S.act(sig_tok[:, j, :], pB1[:, 0:256], AF.Exp, [RpB1], [Rsig], scale=-1.0)
                        _e = sig_tok[:, j, :]
                        sig_finish(_e, _e, [Rsig], [Rsig])
                    for fc2 in range(2):
                        hb_rows = [(0, 64), (64, 128)]
                        for (bank, rb, ti) in ((pA0, RpA0, i1), (pA1, RpA1, i2), (pLN, RpLN, i3)):
                            for j in range(ntl):
                                S.mm(bank[:, j * 128:(j + 1) * 128], sig_tok[:, j, fc2 * 128:(fc2 + 1) * 128], tri_sb[:, ti, :],
                                     True, True, [Rsig, Rp], [rb])
                        S.act(E1[:, fc2, :W], pA0[:, :W], AF.Exp, [RpA0], [RE1])
                        S.act(E2[:, :W], pA0[:, :W], AF.Exp, [RpA0], [RE2], scale=-1.0)
                        S.act(E3[:, :W], pA1[:, :W], AF.Exp, [RpA1], [RE3])
                        S.act(E4[:, :W], pLN[:, :W], AF.Exp, [RpLN], [RE4])
                        c3 = lambda ap: ap.rearrange("p (c t) -> p c t", t=64)
                        for hp in range(2):
                            r0, r1 = hb_rows[hp]
                            S.tt("dve", QRz[hp][r0:r1, fc2, 0:nck, 64:128], c3(rs[r0:r1, fc2, t0:t0 + W]), c3(E1[r0:r1, fc2, :W]),
                                 ALU.mult, [Rrs, RE1], [RQR[hp]])
                            S.tt("pool", QRz[hp][r0:r1, fc2, 0:nck, 0:64], c3(kk[r0:r1, fc2, t0:t0 + W]), c3(E3[r0:r1, :W]),
                                 ALU.mult, [Rkk, RE3], [RQR[hp]])
                        S.tt("dve", kt[:, fc2, :W], kd[:, fc2, :W], E2[:, :W], ALU.mult, [Rkd, RE2], [Rkt])
                        S.tt("pool", bt[:, fc2, :W], bb[:, fc2, :W], E2[:, :W], ALU.mult, [Rbb, RE2], [Rbt])
                        S.tt("dve", khat[:, :W], kd[:, fc2, :W], E4[:, :W], ALU.mult, [Rkd, RE4], [Rkh])
                        S.stt("dve", bhat[:, :W], bb[:, fc2, :W], -1.0, E4[:, :W], ALU.mult, ALU.mult, [Rbb, RE4], [Rbh])
                        for j in range(ntl):
                            for (src, dstl, rl, bank, rb) in ((khat, khat_tok, Rkht, pB0, RpB0), (bhat, bhat_tok, Rbht, pB1, RpB1)):
                                ptb = bank[:, 0:64].bitcast(BF16)
                                S.transpose(ptb, src[:, j * 128:(j + 1) * 128], ident_b[:], [Rkh if src is khat else Rbh, Rid], [rb])
                                S.act(dstl[0][0:64, j, fc2 * 128:(fc2 + 1) * 128], ptb[0:64, :], AF.Identity, [rb], [rl[0]])
                                S.copy("dve", dstl[1][64:128, j, fc2 * 128:(fc2 + 1) * 128], ptb[64:128, :], [rb], [rl[1]])
                    corder = list(range(nck)) if d == 0 else list(range(nck - 1, -1, -1))
                    BANKS = [((pA0, RpA0), (pA1, RpA1), (pLN, RpLN), (pSQ, RpSQ)),
                             ((pXU, RpXU), (pOH, RpOH), (pB0, RpB0), (pB1, RpB1))]

                    def stage_a(ci):
                        c = t0 // 64 + ci
                        pi = c % 2
                        rows = (pi * 64, pi * 64 + 64)
                        R0, R1 = rows
                        cs = slice(ci * 64, ci * 64 + 64)
                        (bA0, rA0), (bA1, rA1), (bLN, rLN), (bSQ, rSQ) = BANKS[pi]
                        for hl in range(4):
                            hp, fc2 = hl % 2, hl // 2
                            S.mm(bA0[R0:R1, hl * 128:(hl + 1) * 128], kt[:, fc2, cs], QRz[hp][:, fc2, ci, :], True, True,
                                 [Rkt, RQR[hp]], [rA0])
                            S.mm(bA1[R0:R1, hl * 128:(hl + 1) * 128], bt[:, fc2, cs], QRz[hp][:, fc2, ci, :], True, True,
                                 [Rbt, RQR[hp]], [rA1])
                            S.mm(bLN[R0:R1, hl * 64:(hl + 1) * 64], QRz[hp][:, fc2, ci, 0:64], bt[:, fc2, cs], True, True,
                                 [Rbt, RQR[hp]], [rLN])
                        S.tt("dve", AT[pi][R0:R1, :, 0:128], v4(bA0, rows, 0, 128),
                             amask_sb[R0:R1, d, 0:128].unsqueeze(1).to_broadcast([64, 4, 128]), ALU.mult, [rA0, Rp, Rz], [RAT[pi]])
                        S.tt("dve", AT[pi][R0:R1, :, 128:256], v4(bA1, rows, 0, 128),
                             amask_sb[R0:R1, d, 128:256].unsqueeze(1).to_broadcast([64, 4, 128]), ALU.mult, [rA1, Rp, Rz], [RAT[pi]])
                        S.tt("dve", Lt[pi][R0:R1, :, :], v4(bLN, rows, 0, 64),
                             lmask_sb[R0:R1, d, :].unsqueeze(1).to_broadcast([64, 4, 64]), ALU.mult, [rLN, Rp, Rz], [RLt[pi]])
                        S.tt("dve", Nt[pi][0][R0:R1, :, :], ident_f[R0:R1, R0:R1].unsqueeze(1).to_broadcast([64, 4, 64]),
                             AT[pi][R0:R1, :, 128:192], ALU.subtract, [RAT[pi], Rid, Rz], [RN[pi][0]])
                        return {"pi": pi, "rows": rows, "ci": ci, "c": c, "nidx": 0, "RL": None}

                    def inv_sq(stt_, lv):
                        pi, rows = stt_["pi"], stt_["rows"]
                        R0, R1 = rows
                        (bA0, rA0), (bA1, rA1), (bLN, rLN), (bSQ, rSQ) = BANKS[pi]
                        last = lv == 4
                        if stt_["RL"] is None:
                            curR = lambda hl: AT[pi][:, hl, 128:192]
                            curL = lambda hl: Lt[pi][:, hl, :]
                            rcur = [RAT[pi], RLt[pi]]
                        else:
                            tlp = stt_["RL"]
                            curR = lambda hl: tlp[:, 0, hl, :]
                            curL = lambda hl: tlp[:, 1, hl, :]
                            rcur = [stt_["RRL"]]
                        for hl in range(4):
                            if not last:
                                S.mm(bSQ[R0:R1, hl * 64:(hl + 1) * 64], curL(hl), curR(hl), True, True, rcur, [rSQ])
                            S.mm(bSQ[R0:R1, 256 + hl * 64:256 + (hl + 1) * 64], curR(hl), curL(hl), True, True, rcur, [rSQ])
                        tl = RLs[pi][lv % 2]
                        if not last:
                            S.act(tl[R0:R1, 0, :, :], v4(bSQ, rows, 0, 64), AF.Identity, [rSQ, Rz], [RRL[pi][lv % 2]])
                        S.copy("dve", tl[R0:R1, 1, :, :], v4(bSQ, rows, 256, 64), [rSQ, Rz], [RRL[pi][lv % 2]])
                        stt_["RL"], stt_["RRL"] = tl, RRL[pi][lv % 2]

                    def inv_prod(stt_, lv):
                        pi, rows = stt_["pi"], stt_["rows"]
                        R0, R1 = rows
                        (bA0, rA0), (bA1, rA1), (bLN, rLN), (bSQ, rSQ) = BANKS[pi]
                        tl, nidx = stt_["RL"], stt_["nidx"]
                        for hl in range(4):
                            S.mm(bLN[R0:R1, 256 + hl * 64:256 + (hl + 1) * 64], tl[:, 1, hl, :], Nt[pi][nidx][:, hl, :], True, True,
                                 [stt_["RRL"], RN[pi][nidx]], [rLN])
                        S.tt("dve", Nt[pi][1 - nidx][R0:R1, :, :], v4(bLN, rows, 256, 64), Nt[pi][nidx][R0:R1, :, :], ALU.add,
                             [rLN, RN[pi][nidx], Rz], [RN[pi][1 - nidx]])
                        stt_["nidx"] = 1 - nidx

                    def seq_stage(stt_):
                        pi, rows, ci, c = stt_["pi"], stt_["rows"], stt_["ci"], stt_["c"]
                        R0, R1 = rows
                        gt, jt = c // 2, ci // 2
                        Nf, RNf = Nt[pi][stt_["nidx"]], RN[pi][stt_["nidx"]]
                        for hl in range(4):
                            hp, fc2 = hl % 2, hl // 2
                            S.mm(pXU[R0:R1, hl * 64:(hl + 1) * 64], QRz[hp][:, fc2, ci, 0:64], Hb[:, fc2, :], True, False,
                                 [RQR[hp], RHb], [RpXU])
                            S.mm(pXU[R0:R1, hl * 64:(hl + 1) * 64], AT[pi][:, hl, 0:64], vt[:, gt, hl * 64:(hl + 1) * 64], False, True,
                                 [RAT[pi], Rvt], [RpXU])
                        S.act(Xs[pi][R0:R1, :, :], v4(pXU, rows, 0, 64), AF.Identity, [RpXU, Rz], [RX[pi]])
                        for hl in range(4):
                            S.mm(pXU[R0:R1, 256 + hl * 64:256 + (hl + 1) * 64], Nf[:, hl, :], Xs[pi][:, hl, :], True, True,
                                 [RNf, RX[pi]], [RpXU])
                        S.copy("dve", Us[pi][R0:R1, :, :], v4(pXU, rows, 256, 64), [RpXU, Rz], [RU[pi]])
                        for hl in range(4):
                            hp, fc2 = hl % 2, hl // 2
                            S.mm(pOH[R0:R1, hl * 64:(hl + 1) * 64], QRz[hp][:, fc2, ci, 64:128], Hb[:, fc2, :], True, False,
                                 [RQR[hp], RHb], [RpOH])
                            S.mm(pOH[R0:R1, hl * 64:(hl + 1) * 64], AT[pi][:, hl, 64:128], vt[:, gt, hl * 64:(hl + 1) * 64], False, False,
                                 [RAT[pi], Rvt], [RpOH])
                            S.mm(pOH[R0:R1, hl * 64:(hl + 1) * 64], AT[pi][:, hl, 192:256], Us[pi][:, hl, :], False, True,
                                 [RAT[pi], RU[pi]], [RpOH])
                        for hl in range(4):
                            hp, fc2 = hl % 2, hl // 2
                            S.mm(pB1[hp * 64:hp * 64 + 64, fc2 * 64:(fc2 + 1) * 64], khat_tok[pi][:, jt, hl * 64:(hl + 1) * 64],
                                 vt[:, gt, hl * 64:(hl + 1) * 64], True, False, [Rkht[pi], Rvt], [RpB1])
                            S.mm(pB1[hp * 64:hp * 64 + 64, fc2 * 64:(fc2 + 1) * 64], bhat_tok[pi][:, jt, hl * 64:(hl + 1) * 64],
                                 Us[pi][:, hl, :], False, True, [Rbht[pi], RU[pi]], [RpB1])
                        if d == 0:
                            S.act(o_acc[R0:R1, gt, :], pOH[R0:R1, 0:256], AF.Identity, [RpOH], [Ro[gt]])
                        else:
                            S.tt("dve", o_acc[R0:R1, gt, :], pOH[R0:R1, 0:256], o_acc[R0:R1, gt, :], ALU.add, [RpOH, Ro[gt]], [Ro[gt]])
                        pcc = ci * 64 + (63 if d == 0 else 0)
                        for fc2 in range(2):
                            S.ts("dve", H[:, fc2, :], H[:, fc2, :], E1[:, fc2, pcc:pcc + 1], None, ALU.mult, None, [RH, RE1], [RH])
                            S.tt("dve", H[:, fc2, :], pB1[:, fc2 * 64:(fc2 + 1) * 64], H[:, fc2, :], ALU.add, [RH, RpB1], [RH])
                        S.act(Hb[:], H[:], AF.Identity, [RH], [RHb])

                    for p0 in range(0, len(corder), 2):
                        grp = corder[p0:p0 + 2]
                        sts = [stage_a(ci) for ci in grp]
                        for lv in range(5):
                            for s_ in sts:
                                inv_sq(s_, lv)
                            for s_ in sts:
                                inv_prod(s_, lv)
                        for s_ in sts:
                            seq_stage(s_)
            if debug:
                S.dma("sp", L["o_dbg"], o_acc[:], Ro, (), "odbg")
                final_keys.append("odbg")
                dbg = {"dAT": (AT[0], BF16), "dN0": (Nt[0][0], BF16), "dN1": (Nt[0][1], BF16), "dQ": (QRz[0], BF16), "dK": (kt, BF16), "dB": (bt, BF16),
                       "dH": (H, F32), "dE1": (E1, F32), "dE2": (E2, F32), "dkd": (kd, F32), "dbb": (bb, F32), "da": (a_sb, F32),
                       "dsig": (sig_tok, F32), "dX": (Xs[0], BF16), "dU": (Us[0], BF16), "dkh": (khat_tok[0], BF16), "dL": (Lt[0], BF16)}
                allres = [Rz, RH, RE1, Ra, Rkd, Rbb, Rsig, Rkt, Rbt] + RQR + RAT + RLt + RX + RU + Rkht + Rbht + [x for y in RN for x in y]
                for nm, (t, dt_) in dbg.items():
                    dd = nc.dram_tensor(nm, list(t.shape), dt_, kind="ExternalOutput").ap()
                    S.dma("sp", dd, t[:], allres, (), "odbg")

            prod = kt
            a2s = tmpb
            bon = sb2("bon", [128, 4]); s1t = sb2("s1t", [128, 4]); s2t = sb2("s2t", [128, 4]); mean = sb2("mean", [128, 4])
            var = sb2("var", [128, 4]); sqt = E2[:, 0:256]; yt = E3[:, 0:256]; bv = E4[:, 0:256]
            yb = sb2("yb", [128, 256], BF16)
            Rprod, Rst, Ry, Ryb, Rrw = Rkt, RE2, RE3, Res(), Res()
            for (t0, W) in (BLOCKS if rlvl >= 6 else []):
                for fc2 in range(2):
                    for d in range(2):
                        S.mm(pB0[:, :W], a2z[:, d, fc2 * 128:(fc2 + 1) * 128], lora6[:, t0:t0 + W], True, True, [Rp, Rl6], [RpB0])
                        adst = a_sb[:, 0, :W] if d == 0 else a2s[:, :W]
                        S.act(adst, pB0[:, :W], AF.Exp, [RpB0, Rp], [Ra], bias=na0[:, fc2, d:d + 1], scale=-1.0)
                        sig_finish(adst, adst, [Ra], [Ra])
                    S.tt("dve", a2s[:, :W], a2s[:, :W], a_sb[:, 0, :W], ALU.add, [Ra], [Ra])
                    S.ts("dve", a2s[:, :W], a2s[:, :W], pp_sb[:, fc2, KEY_A:KEY_A + 1], omka2[:, fc2:fc2 + 1], ALU.mult, ALU.add, [Ra, Rp], [Ra])
                    S.tt("dve", a2s[:, :W], a2s[:, :W], ks[:, fc2, t0:t0 + W], ALU.mult, [Ra, Rks], [Ra])
                    S.stt("dve", prod[:, fc2, :W], a2s[:, :W], pp_sb[:, fc2, BON_U:BON_U + 1], rs[:, fc2, t0:t0 + W], ALU.mult, ALU.mult,
                          [Ra, Rrs, Rp], [Rprod])
                for j in range(W // 128):
                    gt = t0 // 128 + j
                    tk = slice(t0 + j * 128, t0 + (j + 1) * 128)
                    for fc2 in range(2):
                        S.mm(pB1[:, 256 + fc2 * 2:256 + fc2 * 2 + 2], prod[:, fc2, j * 128:(j + 1) * 128], hsel_b[:], True, True, [Rprod, Rp], [RpB1])
                    S.mm(pB1[:, 0:256], sdg[:, tk], g2_sb[:], True, True, [Rsdg, Rp], [RpB1])
                    o3 = o_acc[:, gt, :].rearrange("p (h e) -> p h e", h=4)
                    S.copy("dve", bon[:], pB1[:, 256:260], [RpB1], [Rst])
                    S.op("dve", (lambda o3, s1t: (lambda h: h.reduce_sum(out=s1t[:], in_=o3, axis=AX.X)))(o3, s1t), [Ro[gt]], [Rst])
                    S.act(sqt, o_acc[:, gt, :], AF.Square, [Ro[gt]], [Rst])
                    S.op("dve", (lambda sq3, s2t: (lambda h: h.reduce_sum(out=s2t[:], in_=sq3, axis=AX.X)))(sqt.rearrange("p (h e) -> p h e", h=4), s2t), [Rst], [Rst])
                    S.ts("dve", mean[:], s1t[:], 1.0 / 64, None, ALU.mult, None, [Rst], [Rst])
                    S.tt("dve", var[:], mean[:], mean[:], ALU.mult, [Rst], [Rst])
                    S.stt("dve", var[:], s2t[:], 1.0 / 64, var[:], ALU.mult, ALU.subtract, [Rst], [Rst])
                    S.act(var[:], var[:], AF.Sqrt, [Rst, Rp], [Rst], bias=epsln[:, 0:1])
                    S.recip(var[:], var[:], [Rst], [Rst])
                    y3 = yt.rearrange("p (h e) -> p h e", h=4)
                    S.tt("dve", y3, o3, mean[:].unsqueeze(2).to_broadcast([128, 4, 64]), ALU.subtract, [Ro[gt], Rst], [Ry])
                    S.tt("dve", y3, y3, var[:].unsqueeze(2).to_broadcast([128, 4, 64]), ALU.mult, [Ry, Rst], [Ry])
                    S.tt("pool", yt, yt, lng[:], ALU.mult, [Ry, Rp], [Ry])
                    S.tt("pool", yt, yt, lnb[:], ALU.add, [Ry, Rp], [Ry])
                    S.tt("dve", bv.rearrange("p (h e) -> p h e", h=4), vt[:, gt, :].rearrange("p (h e) -> p h e", h=4),
                         bon[:].unsqueeze(2).to_broadcast([128, 4, 64]), ALU.mult, [Rvt, Rst], [Ry])
                    S.tt("pool", yt, yt, bv, ALU.add, [Ry], [Ry])
                    S.tt("dve", yb[:], yt, pB1[:, 0:256], ALU.mult, [Ry, RpB1], [Ryb])
                    for fc2 in range(2):
                        ptb = pB0[:, 0:64].bitcast(BF16)
                        S.transpose(ptb, yb[:, fc2 * 128:(fc2 + 1) * 128], ident_b[:], [Ryb, Rid], [RpB0])
                        rwst, rrwst, rwi = rwst_r.next()
                        S.act(rwst[:], ptb, AF.Identity, [RpB0], [rrwst])
                        S.dma("sp", rw_o[fc2 * 128:(fc2 + 1) * 128, tk], rwst[:], [rrwst], (), f"rwout{rwi}")
            S.barrier()


NACT = 4


def build_fused():
    nc = bass.Bass("TRN2", target_bir_lowering=False)
    din = lambda name, shape, dt=F32: nc.dram_tensor(name, shape, dt, kind="ExternalInput").ap()
    dout = lambda name, shape, dt=F32: nc.dram_tensor(name, shape, dt, kind="ExternalOutput").ap()
    dscr = lambda name, shape, dt=F32: nc.dram_tensor(name, shape, dt).ap()
    xin = [din(f"xT{h}", [D_MODEL, TOK]) for h in range(2)]
    xout = [dout(f"xo{h}", [D_MODEL, TOK]) for h in range(2)]
    xs = [dscr(f"xs{h}", [D_MODEL, TOK]) for h in range(2)]
    cT = din("cT", [128, 8, 2])
    bm = din("bm", [128, NMODCH])
    w_mod = din("w_mod", [DEPTH, D_MODEL, 9 * D_MODEL])
    ffn_up = din("ffn_up", [DEPTH, 2, D_MODEL, 2 * D_FF])
    ffn_down = din("ffn_down", [DEPTH, 2, D_FF, D_MODEL])
    w_in = din("w_in", [DEPTH, D_MODEL, D_IN])
    w_out = din("w_out", [DEPTH, D_MODEL, D_MODEL])
    ng = din("ng", [DEPTH, 128, 3, 8])
    qkg = din("qkg", [DEPTH, 128, 2])
    nab = din("nab", [DEPTH, 2, 128, 4 * 14 * 64])
    namask = din("namask", [128, 64])
    ident = din("ident", [128, 128])
    tri = din("tri", [128, 512])
    amask = din("amask", [128, 512])
    lmask = din("lmask", [128, 128])
    hsel = din("hsel", [128, 2])
    mu = din("mu", [DEPTH, 2, 128, 8])
    pp = din("pp", [DEPTH, 2, 128, 16])
    w0row = din("w0row", [DEPTH, 2, 1, 512])
    w2 = din("w2", [DEPTH, 2, 64, 512])
    a2 = din("a2", [DEPTH, 2, 64, 512])
    g2 = din("g2", [DEPTH, 2, 128, 256])
    lnrow = din("lnrow", [DEPTH, 2, 1, 512])
    mod_all = dscr("mod_all", [128, 2, NMODCH])
    qT_full = dscr("qT_full", [512, LSEQ], BF16)
    kT_full = dscr("kT_full", [512, LSEQ], BF16)
    v_full = dscr("v_full", [LSEQ, 512], BF16)
    rT_full = dscr("rT_full", [1792, LSEQ])
    mix_full = dscr("mix_full", [D_MODEL, LSEQ], BF16)

    S = Sched(nc)
    emit_mod(nc, S, {"cT": cT, "bm": bm, "w_mod": w_mod, "mod_all": mod_all}, "m_")
    r2 = lambda ap: ap.rearrange("p (a b) -> p a b", a=2)
    for l in range(DEPTH + 1):
        stageA, stageB = l > 0, l < DEPTH
        for half in range(2):
            hs = slice(half * TOK, (half + 1) * TOK)
            io = {"xT_in": xin[half] if l == 0 else xs[half], "xT_out": xout[half] if l == DEPTH else xs[half],
                  "mod_all": mod_all, "lA": l - 1, "lB": l}
            if stageA:
                io.update(mixT=mix_full[:, hs], w_out=w_out[l - 1], up2=ffn_up[l - 1, 1], dn2=ffn_down[l - 1, 1], ngP=ng[l - 1])
            if stageB:
                io.update(up1=ffn_up[l, 0], dn1=ffn_down[l, 0], w_in=w_in[l], qkg=qkg[l], ngC=ng[l],
                          qT_o=qT_full[:, hs], kT_o=kT_full[:, hs], v_o=v_full[hs, :], rT_o=rT_full[:, hs])
            emit_tl(nc, S, io, stageA, stageB, half, f"t{l}{half}_")
        if not stageB:
            break
        for g in range(2):
            def rrow(fc, g=g):
                if fc < 6:
                    base = (fc // 2) * 512 + g * 256 + (fc % 2) * 128
                else:
                    base = 1536 + (fc - 6) * 128
                return rT_full[base:base + 128, :]
            io = {"ident": ident, "qT": qT_full[g * 256:(g + 1) * 256, :], "kT": kT_full[g * 256:(g + 1) * 256, :],
                  "vtok": v_full[:, g * 256:(g + 1) * 256],
                  "nab": nab[l, g].rearrange("p (h d q) -> p h d q", h=4, d=14), "namask": namask,
                  "attn_o": mix_full[g * 256:(g + 1) * 256, :], "rw_o": mix_full[512 + g * 256:512 + (g + 1) * 256, :],
                  "rrow": rrow, "mu": mu[l, g], "pp": r2(pp[l, g]), "w0row": w0row[l, g].rearrange("o (a b) -> o a b", a=2),
                  "w2": r2(w2[l, g]), "a2": r2(a2[l, g]), "g2": g2[l, g], "lnrow": w0row[l, g].rearrange("o (a b) -> o a b", a=2) if False else lnrow[l, g].rearrange("o (a b) -> o a b", a=2),
                  "tri": tri.rearrange("p (a b) -> p a b", a=4), "amask": r2(amask), "lmask": r2(lmask), "hsel": hsel}
            emit_mx(nc, S, io, f"x{l}{g}_")
    S.emit_all(final_keys="all")
    nc._sched_stats = S.stats
    return nc


_PROG = {}


def _rw_params(P, g):
    gs = slice(g * 256, (g + 1) * 256)
    rows = np.concatenate([np.arange(512)[gs], 512 + np.arange(512)[gs], 1024 + np.arange(512)[gs], 1536 + np.arange(256)])
    pm = lambda v: np.ascontiguousarray(v.reshape(-1, 128).T)
    u = P['bonus_u'].reshape(512)
    pp = np.zeros((128, 2, 8), np.float32)
    for i, v in enumerate([P['key_k'][gs], P['key_a'][gs], u[gs], P['iclr_a0'][0][gs], P['iclr_a0'][1][gs]]):
        pp[:, :, i] = pm(v)
    return {
        "mu": pm(P['shift_mu'][rows]),
        "pp": pp.reshape(128, 16),
        "w0row": np.ascontiguousarray(P['decay_w0'][:, gs]).reshape(1, 512),
        "w2": np.ascontiguousarray(P['decay_w2'][:, :, gs].transpose(1, 0, 2)).reshape(64, 512),
        "a2": np.ascontiguousarray(P['iclr_a2'][:, :, gs].transpose(1, 0, 2)).reshape(64, 512),
        "g2": np.ascontiguousarray(P['gate_g2'][:, gs]),
        "lnrow": np.ascontiguousarray(np.stack([P['lnx_gain'][gs], P['lnx_bias'][gs]], 0)).reshape(1, 512),
    }


def _consts():
    e = float(np.exp(np.float32(-0.5)))
    s = np.arange(128)[:, None]; t = np.arange(128)[None, :]
    same = (s // 64) == (t // 64)
    tri = np.stack([same & (s <= t), same & (s < t), same & (s >= t), same & (s > t)], axis=1).astype(np.float32) * (-e)
    s6 = np.arange(64)[:, None]; t6 = np.arange(64)[None, :]
    am, lm = [], []
    for d in range(2):
        st_ = (s6 < t6) if d == 0 else (s6 > t6)
        le_ = (s6 <= t6) if d == 0 else (s6 >= t6)
        am.append(np.concatenate([st_, le_, st_, -1.0 * le_], axis=1).astype(np.float32))
        lm.append(((s6 > t6) if d == 0 else (s6 < t6)).astype(np.float32))
    amask = np.stack(am, axis=1).reshape(64, 512)
    lmask = np.stack(lm, axis=1).reshape(64, 128)
    hsel = np.zeros((128, 2), np.float32); hsel[:64, 0] = 1; hsel[64:, 1] = 1
    return {"tri": np.ascontiguousarray(tri.reshape(128, 512)), "amask": np.ascontiguousarray(np.concatenate([amask, amask], 0)),
            "lmask": np.ascontiguousarray(np.concatenate([lmask, lmask], 0)), "hsel": hsel,
            "ident": np.eye(128, dtype=np.float32)}


def kernel(x, c, ctx, c_ctx, w_mod, b_mod, norm_gain, ffn_up, ffn_down, w_in, q_gain, k_gain, na_bias,
           shift_mu, decay_w0, decay_w2, iclr_a0, iclr_a2, gate_g2, key_k, key_a, bonus_u, lnx_gain,
           lnx_bias, w_out):
    f = lambda a: np.ascontiguousarray(np.asarray(a, dtype=np.float32))
    x, c, ctx, c_ctx, w_mod, b_mod, norm_gain, ffn_up, ffn_down, w_in = map(f, (x, c, ctx, c_ctx, w_mod, b_mod, norm_gain, ffn_up, ffn_down, w_in))
    q_gain, k_gain, na_bias, shift_mu, decay_w0, decay_w2, iclr_a0, iclr_a2 = map(f, (q_gain, k_gain, na_bias, shift_mu, decay_w0, decay_w2, iclr_a0, iclr_a2))
    gate_g2, key_k, key_a, bonus_u, lnx_gain, lnx_bias, w_out = map(f, (gate_g2, key_k, key_a, bonus_u, lnx_gain, lnx_bias, w_out))
    shared = dict(_consts())
    shared.update({"w_mod": w_mod, "ffn_up": ffn_up, "ffn_down": ffn_down, "w_in": w_in, "w_out": w_out})
    shared["bm"] = np.ascontiguousarray(b_mod.reshape(NMODCH, 128).T)
    shared["ng"] = np.ascontiguousarray(norm_gain.reshape(DEPTH, 3, 8, 128).transpose(0, 3, 1, 2))
    shared["qkg"] = np.ascontiguousarray(np.stack([np.tile(q_gain, (1, 2)), np.tile(k_gain, (1, 2))], axis=2))
    nabs, mask = [], None
    for l in range(DEPTH):
        row = []
        for g in range(2):
            tab, mask = _na_tables(na_bias[l], g)
            row.append(tab.reshape(128, 4 * 14 * 64))
        nabs.append(np.stack(row, 0))
    shared["nab"] = np.ascontiguousarray(np.stack(nabs, 0))
    shared["namask"] = mask
    rw = {k: [] for k in ("mu", "pp", "w0row", "w2", "a2", "g2", "lnrow")}
    for l in range(DEPTH):
        P = {"shift_mu": shift_mu[l], "decay_w0": decay_w0[l], "decay_w2": decay_w2[l], "iclr_a0": iclr_a0[l],
             "iclr_a2": iclr_a2[l], "gate_g2": gate_g2[l], "key_k": key_k[l], "key_a": key_a[l], "bonus_u": bonus_u[l],
             "lnx_gain": lnx_gain[l], "lnx_bias": lnx_bias[l]}
        per_g = [_rw_params(P, g) for g in range(2)]
        for k in rw:
            rw[k].append(np.stack([per_g[0][k], per_g[1][k]], 0))
    for k in rw:
        shared[k] = np.ascontiguousarray(np.stack(rw[k], 0))
    seq = np.concatenate([ctx, x], axis=1)
    in_maps = []
    for b in range(NACT):
        m = dict(shared)
        for h in range(2):
            m[f"xT{h}"] = np.ascontiguousarray(seq[b, h * TOK:(h + 1) * TOK].T)
        rows = np.stack([c[b], c_ctx], axis=0)
        m["cT"] = np.ascontiguousarray(rows.reshape(2, 8, 128).transpose(2, 1, 0))
        in_maps.append(m)
    if "f" not in _PROG:
        _PROG["f"] = build_fused()
    res = run_bass_kernel_spmd(_PROG["f"], in_maps, core_ids=list(range(NACT))).results
    out = np.empty((BATCH, SEQ, D_MODEL), np.float32)
    for b in range(BATCH):
        full = np.concatenate([res[b]["xo0"].T, res[b]["xo1"].T], axis=0)
        out[b] = full[CTX:]
    return out
```
